# Optimizing a Trainium2 kernel written in Bass

```python
import functools
import jax, jax.numpy as jnp
from jax import lax
import numpy as np

D_MODEL = 1024
BATCH = 32
SEQ = 256
DEPTH = 1
DEC_BATCH = 8
DEC_SEQ = 4096
PAST_LEN = 512

GRID_W = 64
D_MIX = D_MODEL
D_REC = D_MIX // 2
D_POOL = D_MIX - D_REC
REC_HEADS = 4
REC_DK = D_REC // REC_HEADS
REC_DV = D_REC // REC_HEADS
CHUNK = 32
POOL_WINDOWS = (2, 4, 8, 16)
POOL_GROUPS = len(POOL_WINDOWS)
POOL_GW = D_POOL // POOL_GROUPS
D_IN = 5 * D_REC + D_POOL
D_FF = -(-8 * D_MODEL // (3 * 256)) * 256
DEEPNORM_ALPHA = (2.0 * DEPTH) ** 0.25
DEEPNORM_BETA = (8.0 * DEPTH) ** -0.25
LN_EPS = 1e-5
RMS_EPS = 1e-6

kernel_name = "hymba_hgrn2_pool_dit_step"


def layer_norm(x, g, b):
    xf = x.astype(jnp.float32)
    mu = jnp.mean(xf, axis=-1, keepdims=True)
    var = jnp.mean(jnp.square(xf - mu), axis=-1, keepdims=True)
    return ((xf - mu) * lax.rsqrt(var + LN_EPS) * g + b).astype(x.dtype)


def lower_bounds(raw):
    return jnp.cumsum(jax.nn.softmax(raw.astype(jnp.float32), axis=0), axis=0)


def hgrn2_chunked(q, k, log_f, v, s0):
    bsz, t, h, _ = q.shape
    dv = v.shape[-1]
    n = t // CHUNK

    def blocks(a):
        return a.astype(jnp.float32).reshape(bsz, n, CHUNK, h, a.shape[-1]).transpose(0, 3, 1, 2, 4)

    qc, kc, gc, vc = blocks(q), blocks(k), blocks(log_f), blocks(v)
    b = jnp.cumsum(gc, axis=3)
    b_last = b[:, :, :, -1:]
    q_dec = qc * jnp.exp(b)
    k_inv = kc * jnp.exp(-b)
    k_end = kc * jnp.exp(b_last - b)
    mask = jnp.tril(jnp.ones((CHUNK, CHUNK), dtype=bool))
    scores = jnp.where(mask, jnp.einsum('bhncd,bhnsd->bhncs', q_dec, k_inv), 0.0)
    o_intra = jnp.einsum('bhncs,bhnse->bhnce', scores, vc)
    u = jnp.einsum('bhnsd,bhnse->bhnde', k_end, vc)
    decay = jnp.exp(b_last[:, :, :, 0])

    def step(s, xs):
        dec_n, u_n = xs
        return dec_n[..., None] * s + u_n, s

    s_final, s_starts = lax.scan(step, s0.astype(jnp.float32),
                                 (jnp.moveaxis(decay, 2, 0), jnp.moveaxis(u, 2, 0)))
    s_starts = jnp.moveaxis(s_starts, 0, 2)
    o_inter = jnp.einsum('bhncd,bhnde->bhnce', q_dec, s_starts)
    o = (o_intra + o_inter).transpose(0, 2, 3, 1, 4).reshape(bsz, t, h, dv)
    return o, s_final


def _window_bounds(n, w):
    idx = jnp.arange(n)
    return jnp.clip(idx - w // 2, 0, n), jnp.clip(idx + w // 2, 0, n)


def pool_1d(u, w):
    t = u.shape[1]
    cs = jnp.pad(jnp.cumsum(u.astype(jnp.float32), axis=1), ((0, 0), (1, 0), (0, 0)))
    lo, hi = _window_bounds(t, w)
    total = jnp.take(cs, hi, axis=1) - jnp.take(cs, lo, axis=1)
    return total / (hi - lo).astype(jnp.float32)[None, :, None]


def pool_2d(u, w, rows):
    bsz, t, ch = u.shape
    g = u.astype(jnp.float32).reshape(bsz, rows, GRID_W, ch)
    sat = jnp.pad(jnp.cumsum(jnp.cumsum(g, axis=1), axis=2), ((0, 0), (1, 0), (1, 0), (0, 0)))
    r0, r1 = _window_bounds(rows, w)
    c0, c1 = _window_bounds(GRID_W, w)

    def corner(ri, ci):
        return jnp.take(jnp.take(sat, ri, axis=1), ci, axis=2)

    total = corner(r1, c1) - corner(r0, c1) - corner(r1, c0) + corner(r0, c0)
    cnt = ((r1 - r0)[:, None] * (c1 - c0)[None, :]).astype(jnp.float32)
    return (total / cnt[None, :, :, None]).reshape(bsz, t, ch)


def multi_scale_pool(u, pool_w, pool_scale, pool_fn):
    outs = []
    for gi, w in enumerate(POOL_WINDOWS):
        ug = u[..., gi * POOL_GW:(gi + 1) * POOL_GW]
        diff = (pool_fn(ug, w) - ug.astype(jnp.float32)).astype(u.dtype)
        outs.append(diff @ pool_w[gi])
    return jnp.concatenate(outs, axis=-1) * pool_scale


def hybrid_mixer(h, s_fwd0, s_bwd0, pool_fn, w_in, lb_fwd, lb_bwd, hgrn_norm_g, pool_w, pool_scale, w_out):
    bsz, t, _ = h.shape
    proj = h @ w_in
    q, z_fwd, z_bwd, i_in, g, u_pool = jnp.split(
        proj, [D_REC, 2 * D_REC, 3 * D_REC, 4 * D_REC, 5 * D_REC], axis=-1)
    heads_k = (bsz, t, REC_HEADS, REC_DK)
    q = jax.nn.silu(q).reshape(heads_k)
    v = i_in.reshape(bsz, t, REC_HEADS, REC_DV)

    def gates(z, lb):
        f = lb.reshape(REC_HEADS, REC_DK) + (1.0 - lb.reshape(REC_HEADS, REC_DK)) * jax.nn.sigmoid(
            z.astype(jnp.float32).reshape(heads_k))
        return 1.0 - f, jnp.log(f)

    k_f, log_f_f = gates(z_fwd, lb_fwd)
    k_b, log_f_b = gates(z_bwd, lb_bwd)
    o_f, s_f = hgrn2_chunked(q, k_f, log_f_f, v, s_fwd0)
    o_b_rev, s_b = hgrn2_chunked(q[:, ::-1], k_b[:, ::-1], log_f_b[:, ::-1], v[:, ::-1], s_bwd0)
    o = o_f + o_b_rev[:, ::-1]
    o = o * lax.rsqrt(jnp.mean(jnp.square(o), axis=-1, keepdims=True) + RMS_EPS) * hgrn_norm_g
    y_rec = o.reshape(bsz, t, D_REC).astype(h.dtype) * jax.nn.silu(g)
    y_pool = multi_scale_pool(u_pool, pool_w, pool_scale, pool_fn)
    y = jnp.concatenate([y_rec, y_pool.astype(h.dtype)], axis=-1) @ w_out
    return y, s_f, s_b


def trunk_layer(x, mod, s_fwd0, s_bwd0, pool_fn, w_in, lb_fwd, lb_bwd, hgrn_norm_g, pool_w, pool_scale,
                w_out, ln1_g, ln1_b, w_ffn_gate, w_ffn_up, w_ffn_down, ln2_g, ln2_b):
    shift1, scale1, gate1, shift2, scale2, gate2 = jnp.split(mod, 6, axis=-1)
    h = x * (1.0 + scale1) + shift1
    y, s_f, s_b = hybrid_mixer(h, s_fwd0, s_bwd0, pool_fn, w_in, lb_fwd, lb_bwd, hgrn_norm_g,
                               pool_w, pool_scale, w_out)
    x = layer_norm(DEEPNORM_ALPHA * x + gate1 * y, ln1_g, ln1_b)
    h = x * (1.0 + scale2) + shift2
    ffn = (jax.nn.silu(h @ w_ffn_gate) * (h @ w_ffn_up)) @ w_ffn_down
    x = layer_norm(DEEPNORM_ALPHA * x + gate2 * ffn, ln2_g, ln2_b)
    return x, s_f, s_b


def setup_inputs(seed: int = 0) -> dict:
    key = jax.random.key(seed)
    ks = jax.random.split(key, 22)
    nrm = jax.random.normal
    f32 = jnp.float32
    state_shape = (DEC_BATCH, DEPTH, REC_HEADS, REC_DK, REC_DV)
    return {
        'x_prompt': nrm(ks[0], (BATCH, SEQ, D_MODEL), f32),
        'x_sample': nrm(ks[1], (DEC_BATCH, DEC_SEQ, D_MODEL), f32),
        'state_hgrn_fwd': 0.5 * nrm(ks[2], state_shape, f32),
        'state_hgrn_bwd': 0.5 * nrm(ks[3], state_shape, f32),
        'c': nrm(ks[4], (DEC_BATCH, D_MODEL), f32),
        'c_ctx': nrm(ks[5], (D_MODEL,), f32),
        'w_mod': 0.5 * D_MODEL ** -0.5 * nrm(ks[6], (DEPTH, D_MODEL, 6 * D_MODEL), f32),
        'b_mod': 0.01 * nrm(ks[7], (DEPTH, 6 * D_MODEL), f32),
        'w_in': D_MODEL ** -0.5 * nrm(ks[8], (DEPTH, D_MODEL, D_IN), f32),
        'lb_fwd_raw': 0.1 * nrm(ks[9], (DEPTH + 1, D_REC), f32),
        'lb_bwd_raw': 0.1 * nrm(ks[10], (DEPTH + 1, D_REC), f32),
        'hgrn_norm_g': 1.0 + 0.01 * nrm(ks[11], (DEPTH, REC_DV), f32),
        'pool_w': POOL_GW ** -0.5 * nrm(ks[12], (DEPTH, POOL_GROUPS, POOL_GW, POOL_GW), f32),
        'pool_scale': 1.0 + 0.01 * nrm(ks[13], (DEPTH, D_POOL), f32),
        'w_out': DEEPNORM_BETA * D_MIX ** -0.5 * nrm(ks[14], (DEPTH, D_MIX, D_MODEL), f32),
        'ln1_g': 1.0 + 0.01 * nrm(ks[15], (DEPTH, D_MODEL), f32),
        'ln1_b': 0.01 * nrm(ks[16], (DEPTH, D_MODEL), f32),
        'w_ffn_gate': D_MODEL ** -0.5 * nrm(ks[17], (DEPTH, D_MODEL, D_FF), f32),
        'w_ffn_up': D_MODEL ** -0.5 * nrm(ks[18], (DEPTH, D_MODEL, D_FF), f32),
        'w_ffn_down': DEEPNORM_BETA * D_FF ** -0.5 * nrm(ks[19], (DEPTH, D_FF, D_MODEL), f32),
        'ln2_g': 1.0 + 0.01 * nrm(ks[20], (DEPTH, D_MODEL), f32),
        'ln2_b': 0.01 * nrm(ks[21], (DEPTH, D_MODEL), f32),
    }


def reference(x_prompt, x_sample, state_hgrn_fwd, state_hgrn_bwd, c, c_ctx, w_mod, b_mod, w_in,
              lb_fwd_raw, lb_bwd_raw, hgrn_norm_g, pool_w, pool_scale, w_out, ln1_g, ln1_b,
              w_ffn_gate, w_ffn_up, w_ffn_down, ln2_g, ln2_b):
    rows = x_sample.shape[1] // GRID_W
    pool_lat = functools.partial(pool_2d, rows=rows)
    pool_ctx = pool_1d
    lb_fwd_all = lower_bounds(lb_fwd_raw)
    lb_bwd_all = lower_bounds(lb_bwd_raw)
    zero_state = jnp.zeros((x_prompt.shape[0], REC_HEADS, REC_DK, REC_DV), jnp.float32)
    xp, xs = x_prompt, x_sample
    new_fwd, new_bwd = [], []
    for l in range(DEPTH):
        layer = functools.partial(
            trunk_layer, w_in=w_in[l], lb_fwd=lb_fwd_all[l], lb_bwd=lb_bwd_all[l],
            hgrn_norm_g=hgrn_norm_g[l], pool_w=pool_w[l], pool_scale=pool_scale[l], w_out=w_out[l],
            ln1_g=ln1_g[l], ln1_b=ln1_b[l], w_ffn_gate=w_ffn_gate[l], w_ffn_up=w_ffn_up[l],
            w_ffn_down=w_ffn_down[l], ln2_g=ln2_g[l], ln2_b=ln2_b[l])
        mod_ctx = (jax.nn.silu(c_ctx) @ w_mod[l] + b_mod[l])[None, None, :]
        mod_lat = (jax.nn.silu(c) @ w_mod[l] + b_mod[l])[:, None, :]
        xp, s_f, s_b = layer(xp, mod_ctx, zero_state, zero_state, pool_ctx)
        new_fwd.append(s_f)
        new_bwd.append(s_b)
        xs, _, _ = layer(xs, mod_lat, state_hgrn_fwd[:, l], state_hgrn_bwd[:, l], pool_lat)
    new_state_hgrn_fwd = jnp.stack(new_fwd, axis=1)
    new_state_hgrn_bwd = jnp.stack(new_bwd, axis=1)
    return (xp, xs, new_state_hgrn_fwd, new_state_hgrn_bwd)
```

```python
import numpy as np
import ml_dtypes
from contextlib import ExitStack
import concourse.bass as bass
import concourse.mybir as mybir
from concourse.bass_utils import run_bass_kernel_spmd

F32 = mybir.dt.float32
BF16 = mybir.dt.bfloat16
AF = mybir.ActivationFunctionType
ALU = mybir.AluOpType

NCORES = 8
D = 1024
DFF = 2816
TB = 256
CH = 64
NCHK = TB // CH
LAT_T = 4096
PR_T = 256
NPR = 4
TOK = LAT_T + NPR * PR_T
ALPHA = 2.0 ** 0.25
LN_EPS = 1e-5
RMS_EPS = 1e-6
WINS = (2, 4, 8, 16)
NSLOT = 4


def _pool_tables():
    P = []
    idx2 = {}
    idx1 = {}
    for g, w in enumerate(WINS):
        h = w // 2
        for dl in range(-5, 6):
            m = np.zeros((128, 128), np.float32)
            for a in range(2):
                for a2 in range(2):
                    if a2 - h <= 2 * dl + a < a2 + h:
                        for c2 in range(64):
                            lo, hi = max(c2 - h, 0), min(c2 + h, 64)
                            m[a * 64 + lo:a * 64 + hi, a2 * 64 + c2] = 1.0
            if m.any():
                idx2[(g, dl)] = len(P)
                P.append(m)
        for dl in (-1, 0, 1):
            m = np.zeros((128, 128), np.float32)
            for i2 in range(128):
                lo, hi = i2 - h - 128 * dl, i2 + h - 128 * dl
                lo, hi = max(lo, 0), min(hi, 128)
                if hi > lo:
                    m[lo:hi, i2] = 1.0
            if m.any():
                idx1[(g, dl)] = len(P)
                P.append(m)
    R = []
    rkey = {}
    r2 = {}
    r1 = {}

    def add(v):
        k = v.tobytes()
        if k not in rkey:
            rkey[k] = len(R)
            R.append(v)
        return rkey[k]

    for g, w in enumerate(WINS):
        h = w // 2
        cc = np.array([min(c + h, 64) - max(c - h, 0) for c in range(64)], np.float32)
        for m in range(32):
            v = np.zeros(128, np.float32)
            for a in range(2):
                r = 2 * m + a
                rc = min(r + h, 64) - max(r - h, 0)
                v[a * 64:(a + 1) * 64] = 1.0 / (rc * cc)
            r2[(g, m)] = add(v)
        for m in range(2):
            t = np.arange(128) + 128 * m
            cnt = np.minimum(t + h, 256) - np.maximum(t - h, 0)
            r1[(g, m)] = add((1.0 / cnt).astype(np.float32))
    PT = np.stack(P, 1)
    RC = np.ascontiguousarray(np.stack(R, 1))
    return PT.astype(ml_dtypes.bfloat16), RC.astype(np.float32), idx2, idx1, r2, r1


_PT, _RC, _IDX2, _IDX1, _R2, _R1 = _pool_tables()
NPT = _PT.shape[1]
NRC = _RC.shape[1]


def _const_f32():
    ident = np.eye(128, dtype=np.float32)
    p = np.arange(128) % 64
    c = np.arange(64)
    mF = np.where(c[None, :] >= p[:, None], -1.0, 0.0).astype(np.float32)
    mB = np.where(c[None, :] <= p[:, None], -1.0, 0.0).astype(np.float32)
    mF = np.repeat(mF[:, None, :], 2, 1).reshape(128, 128)
    mB = np.repeat(mB[:, None, :], 2, 1).reshape(128, 128)
    st = np.zeros((128, TB), np.float32)
    st[:, ::CH] = 1.0
    en = np.zeros((128, TB), np.float32)
    en[:, CH - 1::CH] = 1.0
    return np.concatenate([ident, mF, mB, st, en, 1.0 - st, 1.0 - en], 1)


_CF = _const_f32()
CF_ID, CF_MF, CF_MB, CF_ST, CF_EN, CF_NST, CF_NEN = 0, 128, 256, 384, 384 + TB, 384 + 2 * TB, 384 + 3 * TB
PRM_LBF, PRM_LBB, PRM_GN, PRM_PS, PRM_L1G, PRM_L1B, PRM_L2G, PRM_L2B, PRM_N = 0, 8, 16, 17, 21, 29, 37, 45, 53


class Buf:
    __slots__ = ("w", "r", "name")

    def __init__(self, name=""):
        self.w = None
        self.r = {}
        self.name = name


class Sched:
    def __init__(self, nc, es):
        self.nc = nc
        self.eng = {"pe": nc.tensor, "act": nc.scalar, "dve": nc.vector, "pool": nc.gpsimd, "sp": nc.sync}
        self.sems = {}
        self.cnt = {}
        self.waited = {e: {} for e in self.eng}
        self.es = es
        for e in self.eng:
            self.sems[e] = es.enter_context(nc.semaphore("prog_" + e))
            self.cnt[e] = 0
        self.nwait = 0

    def new_sem(self, name):
        s = self.es.enter_context(self.nc.semaphore(name))
        self.sems[name] = s
        self.cnt[name] = 0
        return name

    def _deps(self, e, reads, writes):
        deps = {}

        def add(tok):
            if tok is None:
                return
            k, v = tok
            if k == e and e == "pe":
                return
            if deps.get(k, 0) < v:
                deps[k] = v

        for b in reads:
            add(b.w)
        for b in writes:
            add(b.w)
            for k, v in b.r.items():
                add((k, v))
        out = []
        for k, v in deps.items():
            if self.waited[e].get(k, 0) >= v:
                continue
            self.waited[e][k] = v
            out.append((k, v))
        return out

    def _emit(self, e, fn, deps):
        eng = self.eng[e]
        for k, v in deps[1:]:
            eng.wait_ge(self.sems[k], v)
            self.nwait += 1
        ins = fn()
        if deps:
            ins._wait_ge(self.sems[deps[0][0]], deps[0][1])
        return ins

    def op(self, e, fn, reads=(), writes=()):
        deps = self._deps(e, reads, writes)
        ins = self._emit(e, fn, deps)
        self.cnt[e] += 1
        ins.then_inc(self.sems[e], 1)
        tok = (e, self.cnt[e])
        for b in reads:
            if b.r.get(e, 0) < tok[1]:
                b.r[e] = tok[1]
        for b in writes:
            b.w = tok
            b.r = {}
        return ins

    def dma(self, q, sem, out, in_, reads=(), writes=(), **kw):
        deps = self._deps(q, reads, writes)
        eng = self.eng[q]
        ins = self._emit(q, lambda: eng.dma_start(out=out, in_=in_, **kw), deps)
        self.cnt[sem] += 16
        ins.then_inc(self.sems[sem], 16)
        tok = (sem, self.cnt[sem])
        for b in reads:
            if b.r.get(sem, 0) < tok[1]:
                b.r[sem] = tok[1]
        for b in writes:
            b.w = tok
            b.r = {}
        return ins


def build_program(order=None):
    nc = bass.Bass("TRN2", target_bir_lowering=False)

    def din(name, shape, dt=F32):
        return nc.dram_tensor(name, list(shape), dt, kind="ExternalInput").ap()

    xs = din("xs", [TOK, D])
    st0 = din("st0", [128, 2, 4, 128])
    cvec = din("cvec", [128, 8, 2])
    w_mod = din("w_mod", [D, 6 * D])
    b_mod = din("b_mod", [128, 48])
    w_in = din("w_in", [D, 3 * D])
    w_out = din("w_out", [D, D])
    wg = din("wg", [D, DFF])
    wu = din("wu", [D, DFF])
    wd = din("wd", [DFF, D])
    poolw = din("poolw", [128, 4, 128])
    prm = din("prm", [128, PRM_N])
    cf = din("cf", list(_CF.shape))
    ptab = din("ptab", [128, NPT, 128], BF16)
    rtab = din("rtab", [128, NRC])
    y = nc.dram_tensor("y", [TOK, D], F32, kind="ExternalOutput").ap()
    nsf = nc.dram_tensor("nsf", [NPR, 4, 128, 128], F32, kind="ExternalOutput").ap()
    nsb = nc.dram_tensor("nsb", [NPR, 4, 128, 128], F32, kind="ExternalOutput").ap()
    w_in_b = nc.dram_tensor("w_in_b", [D, 3 * D], BF16).ap()
    w_out_b = nc.dram_tensor("w_out_b", [D, D], BF16).ap()
    wg_b = nc.dram_tensor("wg_b", [D, DFF], BF16).ap()
    wu_b = nc.dram_tensor("wu_b", [D, DFF], BF16).ap()
    wd_b = nc.dram_tensor("wd_b", [DFF, D], BF16).ap()
    ups = nc.dram_tensor("ups", [LAT_T, 512], BF16).ap()
    bst = nc.dram_tensor("bst", [LAT_T // TB, 128, 4, 128], F32).ap()

    es = ExitStack()
    with es:
        S = Sched(nc, es)

        def sb(name, shape, dt=F32):
            return es.enter_context(nc.sbuf_tensor(name, list(shape), dt))

        def bufs(n, name):
            return [Buf(f"{name}{i}") for i in range(n)]

        cft = sb("cft", _CF.shape)
        identb = sb("identb", [128, 128], BF16)
        onesb = sb("onesb", [128, 2, 128], BF16)
        ptt = sb("ptt", [128, NPT, 128], BF16)
        rct = sb("rct", [128, NRC])
        lnc1 = sb("lnc1", [128, 2, 4])
        epsc = sb("epsc", [128, 2])
        prmt = sb("prmt", [128, PRM_N])
        poolwf = sb("poolwf", [128, 4, 128])
        poolwb = sb("poolwb", [128, 4, 128], BF16)
        cvt = sb("cvt", [128, 16])
        scv = sb("scv", [128, 16])
        bmt = sb("bmt", [128, 48])
        modv = sb("modv", [128, 48, 2])
        der = sb("der", [128, 2, 3, 8])
        shr = sb("shr", [128, 2, 8])
        lbc = sb("lbc", [128, 2, 2, 4])
        lbt = sb("lbt", [128, 2, 4])
        nps = sb("nps", [128, 4])
        B_const = Buf("const")

        slots = [sb(f"wslot{i}", [128, 2048]) for i in range(NSLOT)]
        slot_b = bufs(NSLOT, "wslot")
        slot_sem = [S.new_sem(f"wsem{i}") for i in range(NSLOT)]

        xin = [sb(f"xin{i}", [128, D]) for i in range(2)]
        xin_b = bufs(2, "xin")
        xin_sem = [S.new_sem(f"xsem{i}") for i in range(2)]
        xTa = sb("xTa", [128, 8, TB])
        xTa_b = bufs(8, "xTa")
        hT = sb("hT", [128, 8, TB], BF16)
        hT_b = bufs(8, "hT")
        tb = sb("tb", [128, 8, TB])
        tb_b = bufs(8, "tb")
        gw = sb("gw", [128, 6, TB])
        gw_b = bufs(6, "gw")
        sq = [sb(f"sq{i}", [128, TB]) for i in range(4)]
        sq_b = bufs(4, "sq")
        edg = sb("edg", [128, 2, 4, NCHK])
        edg_b = [bufs(4, f"edg{d}_") for d in range(2)]
        qh = sb("qh", [128, 2, 4, TB], BF16)
        qh_b = [bufs(4, f"qh{d}_") for d in range(2)]
        kh = sb("kh", [128, 2, 4, TB], BF16)
        kh_b = [bufs(4, f"kh{d}_") for d in range(2)]
        khT = sb("khT", [128, 2, 4, 2, 128], BF16)
        khT_b = [bufs(4, f"khT{d}_") for d in range(2)]
        vT = sb("vT", [128, 2, 512], BF16)
        vT_b = bufs(2, "vT")
        sgate = sb("sgate", [128, 4, TB], BF16)
        sgate_b = bufs(4, "sgate")
        scs = sb("scs", [128, 2, 4, 2, CH], BF16)
        scs_b = [bufs(4, f"scs{d}_") for d in range(2)]
        Tst = sb("Tst", [128, 2, 4, 2, 128])
        Tst_b = [[bufs(2, f"T{d}{h}_") for h in range(4)] for d in range(2)]
        Sbb = sb("Sbb", [128, 4, NCHK, 128], BF16)
        Sbb_b = [bufs(NCHK, f"Sbb{h}_") for h in range(4)]
        Sbf = sb("Sbf", [128, 4, 2, 128], BF16)
        Sbf_b = [bufs(2, f"Sbf{h}_") for h in range(4)]
        carryF = sb("carryF", [128, 4, 128])
        carryF_b = bufs(4, "carryF")
        carryB = sb("carryB", [128, 4, 128])
        carryB_b = bufs(4, "carryB")
        binit = sb("binit", [128, 4, 128])
        binit_b = Buf("binit")
        binit_sem = S.new_sem("binit_sem")
        zst = sb("zst", [128, 128])
        osb = [sb(f"osb{i}", [128, TB]) for i in range(4)]
        osb_b = bufs(4, "osb")
        osq = [sb(f"osq{i}", [128, TB], BF16) for i in range(2)]
        osq_b = bufs(2, "osq")
        rstd_t = sb("rstd_t", [128, 4, TB])
        rstd_b = Buf("rstd")
        ymix = sb("ymix", [128, 8, TB], BF16)
        ymix_b = bufs(8, "ymix")
        uwin = sb("uwin", [128, 10, 512], BF16)
        uwin_b = Buf("uwin")
        uwin_sem = S.new_sem("uwin_sem")
        dtm = [sb(f"dtm{i}", [128, 2, 128], BF16) for i in range(2)]
        dtm_b = bufs(2, "dtm")
        ndf = [sb(f"ndf{i}", [128, TB], BF16) for i in range(2)]
        ndf_b = bufs(2, "ndf")
        rr = sb("rr", [128, 8, TB])
        rr_b = bufs(8, "rr")
        rbb = sb("rbb", [128, 8, TB], BF16)
        rbb_b = bufs(8, "rbb")
        rsq = sb("rsq", [128, 8, TB], BF16)
        rsq_b = bufs(8, "rsq")
        stt = sb("stt", [128, 4, TB])
        stt_b = bufs(4, "stt")
        x2a = sb("x2a", [128, 8, TB])
        x2a_b = bufs(8, "x2a")
        h2 = sb("h2", [128, 8, TB], BF16)
        h2_b = bufs(8, "h2")
        act = sb("act", [128, 22, TB], BF16)
        act_b = bufs(22, "act")
        sgt = [sb(f"sgt{i}", [128, TB]) for i in range(2)]
        sgt_b = bufs(2, "sgt")
        outF = sb("outF", [128, 8, TB])
        outF_b = bufs(8, "outF")
        actf = act[:].rearrange("p a b -> p (a b)").bitcast(F32)
        stg = [actf[:, i * D:(i + 1) * D] for i in range(2)]
        stg_b = bufs(2, "stg")
        stg_sem = [S.new_sem(f"stgsem{i}") for i in range(2)]
        misc_sem = S.new_sem("misc_sem")
        ups_sem = S.new_sem("ups_sem")
        bst_sem = S.new_sem("bst_sem")
        nsf_sem = S.new_sem("nsf_sem")
        nsb_sems = [S.new_sem(f"nsb_sem{i}") for i in range(2)]
        cast_sems = {k: S.new_sem("cast_" + k) for k in ("in", "out", "g", "u", "d")}
        B_ups = Buf("ups_dram")
        B_bst = [Buf(f"bst{i}") for i in range(LAT_T // TB)]
        B_wb = {k: Buf("wb_" + k) for k in ("in", "out", "g", "u", "d")}
        upT = scs[:].rearrange("p a b c d -> p (a b c d)").rearrange("p (j f) -> p j f", f=512)
        scs_all = scs_b[0] + scs_b[1]
        upT_b = [scs_all, scs_all]
        nsst = [carryB, binit]

        def nsst_hb(ni, h):
            return carryB_b[h] if ni == 0 else binit_b

        psum = [es.enter_context(nc.psum_tensor(f"ps{i}", [128, 512], F32)) for i in range(8)]
        ps_b = bufs(8, "ps")
        pctr = [0]

        live = set()

        def bank():
            for _ in range(8):
                i = pctr[0] % 8
                pctr[0] += 1
                if i not in live:
                    live.add(i)
                    return psum[i], ps_b[i]
            raise RuntimeError("no free PSUM bank")

        def done(pb):
            live.discard(ps_b.index(pb))

        block = es.enter_context(nc.Block())
        V, A, G, PE, SP = nc.vector, nc.scalar, nc.gpsimd, nc.tensor, nc.sync

        def cast(dst, src, rows, cols, key):
            for c0 in range(0, cols, 1024):
                c1 = min(cols, c0 + 1024)
                for r0 in range(0, rows, 512):
                    r1 = min(rows, r0 + 512)
                    S.dma("pool", cast_sems[key], dst[r0:r1, c0:c1], src[r0:r1, c0:c1], writes=[B_wb[key]])

        cast(w_in_b, w_in, D, 3 * D, "in")

        S.dma("sp", misc_sem, cft[:], cf[:], writes=[B_const])
        S.dma("sp", misc_sem, ptt[:], ptab[:], writes=[B_const])
        S.dma("sp", misc_sem, rct[:], rtab[:], writes=[B_const])
        S.dma("sp", misc_sem, prmt[:], prm[:], writes=[B_const])
        S.dma("sp", misc_sem, poolwf[:], poolw[:], writes=[B_const])
        S.dma("sp", misc_sem, cvt[:], cvec.rearrange("p a b -> p (a b)"), writes=[B_const])
        S.dma("sp", misc_sem, bmt[:], b_mod[:], writes=[B_const])
        B_c2 = Buf("const2")
        S.op("dve", lambda: V.tensor_copy(out=identb[:], in_=cft[:, CF_ID:CF_ID + 128]), reads=[B_const], writes=[B_c2])
        S.op("dve", lambda: V.memset(onesb[:, 0, :], 1.0 / 128.0), writes=[B_c2])
        S.op("dve", lambda: V.memset(onesb[:, 1, :], 1.0 / 1024.0), writes=[B_c2])
        S.op("dve", lambda: V.memset(zst[:], 0.0), writes=[B_c2])
        S.op("dve", lambda: V.tensor_copy(out=poolwb[:], in_=poolwf[:]), reads=[B_const], writes=[B_c2])
        S.op("act", lambda: A.activation(out=scv[:], in_=cvt[:], func=AF.Silu), reads=[B_const], writes=[B_c2])
        for d_, col in ((0, PRM_LBF), (1, PRM_LBB)):
            rv = prmt[:, col:col + 8].rearrange("p (h r) -> p h r", r=2)
            S.op("dve", lambda rv=rv, d_=d_: V.tensor_tensor(out=lbt[:, d_, :], in0=rv[:, :, 0], in1=rv[:, :, 1], op=ALU.subtract),
                 reads=[B_const], writes=[B_c2])
            S.op("act", lambda d_=d_: A.activation(out=lbt[:, d_, :], in_=lbt[:, d_, :], func=AF.Tanh, scale=0.5),
                 reads=[B_c2], writes=[B_c2])
            S.op("dve", lambda d_=d_: V.tensor_scalar(out=lbc[:, d_, 0, :], in0=lbt[:, d_, :], scalar1=-0.25, scalar2=0.25,
                                                    op0=ALU.mult, op1=ALU.add), reads=[B_c2], writes=[B_c2])
            S.op("dve", lambda d_=d_: V.tensor_scalar(out=lbc[:, d_, 1, :], in0=lbt[:, d_, :], scalar1=0.25, scalar2=0.75,
                                                    op0=ALU.mult, op1=ALU.add), reads=[B_c2], writes=[B_c2])
        for d_ in range(2):
            S.op("act", lambda d_=d_: A.activation(out=lnc1[:, d_, :], in_=lbc[:, d_, 0, :], func=AF.Ln), reads=[B_c2], writes=[B_c2])
        S.op("dve", lambda: V.memset(epsc[:, 0:1], RMS_EPS), writes=[B_c2])
        S.op("dve", lambda: V.memset(epsc[:, 1:2], LN_EPS), writes=[B_c2])
        S.op("dve", lambda: V.tensor_copy(out=nps[:], in_=prmt[:, PRM_PS:PRM_PS + 4]), reads=[B_const], writes=[B_c2])
        S.op("dve", lambda: V.tensor_scalar(out=shr[:, 0, :], in0=prmt[:, PRM_L1G:PRM_L1G + 8], scalar1=ALPHA, scalar2=None, op0=ALU.mult),
             reads=[B_const], writes=[B_c2])
        S.op("dve", lambda: V.tensor_scalar(out=shr[:, 1, :], in0=prmt[:, PRM_L1B:PRM_L1B + 8], scalar1=ALPHA, scalar2=None, op0=ALU.mult),
             reads=[B_const], writes=[B_c2])

        mps, mps_b = bank()
        scv3 = scv[:].rearrange("p (a b) -> p a b", b=2)
        def mod_load(u):
            si = u % NSLOT
            S.dma("sp", slot_sem[si], slots[si][:].rearrange("p (k c) -> p k c", c=256),
                  w_mod[:, u * 256:(u + 1) * 256].rearrange("(k p) c -> p k c", p=128), writes=[slot_b[si]])

        for u in range(3):
            mod_load(u)
        for u in range(24):
            si = u % NSLOT
            if u + 3 < 24:
                mod_load(u + 3)
            wv = slots[si][:].rearrange("p (k c) -> p k c", c=256)
            for jj in range(2):
                j = u * 2 + jj
                for kc in range(8):
                    S.op("pe", lambda j=j, kc=kc, jj=jj, wv=wv: PE.matmul(
                        mps[:, j * 2:(j + 1) * 2], lhsT=wv[:, kc, jj * 128:(jj + 1) * 128], rhs=scv3[:, kc, :],
                        start=(kc == 0), stop=(kc == 7)), reads=[slot_b[si], B_c2], writes=[mps_b])
        mp3 = mps[:, 0:96].rearrange("p (j w) -> p j w", w=2)
        for w_ in range(2):
            S.op("dve", lambda w_=w_: V.tensor_tensor(out=modv[:, :, w_], in0=mp3[:, :, w_], in1=bmt[:], op=ALU.add),
                 reads=[B_const], writes=[mps_b, B_c2])
            if w_ == 1:
                done(mps_b)
            S.op("dve", lambda w_=w_: V.tensor_scalar(out=der[:, w_, 0, :], in0=modv[:, 8:16, w_], scalar1=1.0, scalar2=None, op0=ALU.add),
                 reads=[B_c2], writes=[B_c2])
            S.op("dve", lambda w_=w_: V.tensor_scalar(out=der[:, w_, 2, :], in0=modv[:, 32:40, w_], scalar1=1.0, scalar2=None, op0=ALU.add),
                 reads=[B_c2], writes=[B_c2])
            S.op("dve", lambda w_=w_: V.tensor_tensor(out=der[:, w_, 1, :], in0=der[:, w_, 2, :], in1=prmt[:, PRM_L1G:PRM_L1G + 8], op=ALU.mult),
                 reads=[B_c2], writes=[B_c2])
            S.op("dve", lambda w_=w_: V.tensor_tensor(out=der[:, w_, 2, :], in0=der[:, w_, 2, :], in1=prmt[:, PRM_L1B:PRM_L1B + 8], op=ALU.mult),
                 reads=[B_c2], writes=[B_c2])
            S.op("dve", lambda w_=w_: V.tensor_tensor(out=der[:, w_, 2, :], in0=der[:, w_, 2, :], in1=modv[:, 24:32, w_], op=ALU.add),
                 reads=[B_c2], writes=[B_c2])

        cast(w_out_b, w_out, D, D, "out")
        cast(wg_b, wg, D, DFF, "g")
        cast(wu_b, wu, D, DFF, "u")
        cast(wd_b, wd, DFF, D, "d")

        def spec_of(ident):
            kind_ = ident[0]
            if kind_ == "in":
                u = ident[1]
                return (w_in_b[:, u * 512:(u + 1) * 512].rearrange("(k p) c -> p k c", p=128), (8, 512), "in")
            if kind_ == "out":
                u = ident[1]
                return (w_out_b[:, u * 512:(u + 1) * 512].rearrange("(k p) c -> p k c", p=128), (8, 512), "out")
            if kind_ in ("g", "u"):
                u = ident[1]
                wb_ = wg_b if kind_ == "g" else wu_b
                c0 = u * 512
                c1 = min(DFF, c0 + 512)
                return (wb_[:, c0:c1].rearrange("(k p) c -> p k c", p=128), (8, c1 - c0), kind_)
            jp, hf = ident[1], ident[2]
            return (wd_b[hf * 1408:(hf + 1) * 1408, jp * 256:(jp + 1) * 256].rearrange("(k p) c -> p k c", p=128), (11, 256), "d")

        recording = order is None
        wq = [] if recording else list(order)
        wstate = {"issued": 0, "used": 0}

        def w_issue():
            while wstate["issued"] < len(wq) and wstate["issued"] < max(wstate["used"] - 1, 0) + NSLOT - 1:
                i = wstate["issued"]
                src_, (kk, cc), key = spec_of(wq[i])
                si = i % NSLOT
                dst = slots[si][:].bitcast(BF16)[:, 0:kk * cc].rearrange("p (k c) -> p k c", c=cc)
                S.dma("sp", slot_sem[si], dst, src_, reads=[B_wb[key]], writes=[slot_b[si]])
                wstate["issued"] += 1

        def w_next(*ident):
            i = wstate["used"]
            if recording:
                wq.append(ident)
            assert tuple(wq[i]) == tuple(ident), (i, wq[i], ident)
            src_, (kk, cc), key = spec_of(wq[i])
            wstate["used"] += 1
            w_issue()
            assert wstate["issued"] > i
            si = i % NSLOT
            return slots[si][:].bitcast(BF16)[:, 0:kk * cc].rearrange("p (k c) -> p k c", c=cc), slot_b[si]

        NLB = LAT_T // TB
        plan = []
        for b in reversed(range(NLB)):
            plan.append(("pre", b))
        for b in range(NLB):
            plan.append(("lat", b))
        for s_ in range(NPR):
            plan.append(("pr", s_))

        def issue_x(tok0):
            for j in range(2):
                S.dma("sp", xin_sem[j], xin[j][:], xs[tok0 + j * 128: tok0 + (j + 1) * 128, :], writes=[xin_b[j]])

        def load_x(tok0, nxt_tok0):
            pst = []
            for kp in range(4):
                ps, pb = bank()
                for kk in range(2):
                    kc = 2 * kp + kk
                    for j in range(2):
                        S.op("pe", lambda ps=ps, kk=kk, j=j, kc=kc: PE.transpose(
                            ps[:, kk * 256 + j * 128: kk * 256 + (j + 1) * 128], xin[j][:, kc * 128:(kc + 1) * 128],
                            cft[:, CF_ID:CF_ID + 128]), reads=[xin_b[j], B_const], writes=[pb])
                pst.append((ps, pb))
            if nxt_tok0 is not None:
                issue_x(nxt_tok0)
            return pst

        def evac_x(pst, w_, want_xta):
            for kp in range(4):
                ps, pb = pst[kp]
                for kk in range(2):
                    kc = 2 * kp + kk
                    if want_xta or kc % 2 == 0:
                        S.op("act", lambda ps=ps, kk=kk, kc=kc: A.activation(
                            out=hT[:, kc, :], in_=ps[:, kk * 256:(kk + 1) * 256], func=AF.Identity,
                            scale=der[:, w_, 0, kc:kc + 1], bias=modv[:, kc, w_:w_ + 1]), reads=[B_c2], writes=[pb, hT_b[kc]])
                    else:
                        S.op("dve", lambda ps=ps, kk=kk, kc=kc: V.tensor_scalar(
                            out=hT[:, kc, :], in0=ps[:, kk * 256:(kk + 1) * 256], scalar1=der[:, w_, 0, kc:kc + 1], scalar2=modv[:, kc, w_:w_ + 1],
                            op0=ALU.mult, op1=ALU.add), reads=[B_c2], writes=[pb, hT_b[kc]])
                    if want_xta:
                        S.op("dve", lambda ps=ps, kk=kk, kc=kc: V.tensor_scalar(
                            out=xTa[:, kc, :], in0=ps[:, kk * 256:(kk + 1) * 256], scalar1=ALPHA, scalar2=None, op0=ALU.mult),
                            writes=[pb, xTa_b[kc]])
                done(pb)

        def proj_fm(wv, wb, jc, ps_ap, pb):
            for kc in range(8):
                S.op("pe", lambda kc=kc: PE.matmul(ps_ap, lhsT=wv[:, kc, jc * 128:(jc + 1) * 128], rhs=hT[:, kc, :],
                                                  start=(kc == 0), stop=(kc == 7)), reads=[wb, hT_b[kc]], writes=[pb])

        def proj_tm(wv, wb, dst, dst_b, eng="act"):
            for j in range(2):
                ps, pb = bank()
                for kc in range(8):
                    S.op("pe", lambda kc=kc, j=j, ps=ps: PE.matmul(ps[:, :], lhsT=hT[:, kc, j * 128:(j + 1) * 128], rhs=wv[:, kc, :],
                                                                start=(kc == 0), stop=(kc == 7)), reads=[wb, hT_b[kc]], writes=[pb])
                db = dst_b[j] if isinstance(dst_b, list) else dst_b
                dbl = db if isinstance(db, list) else [db]
                if eng == "act":
                    S.op("act", lambda j=j, ps=ps: A.activation(out=dst[:, j, :], in_=ps[:, :], func=AF.Copy), writes=[pb] + dbl)
                else:
                    S.op("dve", lambda j=j, ps=ps: V.tensor_copy(out=dst[:, j, :], in_=ps[:, :]), writes=[pb] + dbl)
                done(pb)

        gctr = [0]

        def gates_tanh(ps_ap, pb, h, d_):
            i = d_ * 4 + h
            S.op("act", lambda: A.activation(out=tb[:, i, :], in_=ps_ap, func=AF.Tanh, scale=0.5), writes=[pb, tb_b[i]])
            done(pb)

        gslot = {}

        def gates_ln(h, d_):
            i = d_ * 4 + h
            k0 = (3 * gctr[0]) % 6
            gctr[0] += 1
            s_l = k0
            gslot[(h, d_)] = (k0, k0 + 1, k0 + 2)
            S.op("act", lambda: A.activation(out=gw[:, s_l, :], in_=tb[:, i, :], func=AF.Ln, scale=lbc[:, d_, 0, h:h + 1], bias=lbc[:, d_, 1, h:h + 1]),
                 reads=[tb_b[i], B_c2], writes=[gw_b[s_l]])

        def gates_all(pairs, want_q):
            gates_ln(*pairs[0])
            for n_, (h, d_) in enumerate(pairs):
                if n_ + 1 < len(pairs):
                    gates_ln(*pairs[n_ + 1])
                gates_chain(h, d_, want_q)

        def gates_chain(h, d_, want_q):
            i = d_ * 4 + h
            s_l, s_b, s_r = gslot[(h, d_)]
            if d_ == 0:
                mk = cft[:, CF_NST:CF_NST + TB]
                S.op("dve", lambda: V.tensor_tensor_scan(out=gw[:, s_b, :], data0=mk, data1=gw[:, s_l, :], initial=0.0,
                                                         op0=ALU.mult, op1=ALU.add), reads=[gw_b[s_l], B_const], writes=[gw_b[s_b]])
            else:
                mk = cft[:, CF_NEN:CF_NEN + TB]
                S.op("dve", lambda: V.tensor_tensor_scan(out=gw[:, s_b, ::-1], data0=mk[:, ::-1], data1=gw[:, s_l, ::-1], initial=0.0,
                                                         op0=ALU.mult, op1=ALU.add), reads=[gw_b[s_l], B_const], writes=[gw_b[s_b]])
            S.op("act", lambda: A.activation(out=gw[:, s_l, :], in_=gw[:, s_b, :], func=AF.Exp), reads=[gw_b[s_b]], writes=[gw_b[s_l]])
            S.op("act", lambda: A.activation(out=gw[:, s_r, :], in_=gw[:, s_b, :], func=AF.Exp, scale=-1.0, bias=lnc1[:, d_, h:h + 1]),
                 reads=[gw_b[s_b], B_c2], writes=[gw_b[s_r]])
            ecols = gw[:, s_l, CH - 1::CH] if d_ == 0 else gw[:, s_l, 0::CH]
            S.op("pool", lambda: G.tensor_copy(out=edg[:, d_, h, :], in_=ecols), reads=[gw_b[s_l]], writes=[edg_b[d_][h]])
            S.op("dve", lambda: V.scalar_tensor_tensor(out=kh[:, d_, h, :], in0=tb[:, i, :], scalar=1.0, in1=gw[:, s_r, :],
                                                       op0=ALU.subtract, op1=ALU.mult), reads=[tb_b[i], gw_b[s_r]], writes=[kh_b[d_][h]])
            if want_q:
                S.op("pool", lambda: G.tensor_tensor(out=qh[:, d_, h, :], in0=sq[h][:], in1=gw[:, s_l, :], op=ALU.mult),
                     reads=[sq_b[h], gw_b[s_l]], writes=[qh_b[d_][h]])

        def khT_make(h, d_):
            ps, pb = bank()
            psb = ps[:].bitcast(BF16)
            for j in range(2):
                S.op("pe", lambda j=j: PE.transpose(psb[:, j * 128:(j + 1) * 128], kh[:, d_, h, j * 128:(j + 1) * 128], identb[:]),
                     reads=[kh_b[d_][h], B_c2], writes=[pb])
            S.op("act", lambda: A.activation(out=khT[:, d_, h, :, :], in_=psb[:, 0:256].rearrange("p (j c) -> p j c", c=128), func=AF.Copy),
                 writes=[pb, khT_b[d_][h]])
            done(pb)

        def scores(h, d_):
            ps, pb = bank()
            for m in range(NCHK):
                j, a = m // 2, m % 2
                S.op("pe", lambda m=m, j=j, a=a: PE.matmul(
                    ps[a * 64:(a + 1) * 64, j * 64:(j + 1) * 64], lhsT=kh[:, d_, h, m * 64:(m + 1) * 64], rhs=qh[:, d_, h, m * 64:(m + 1) * 64],
                    start=True, stop=True), reads=[kh_b[d_][h], qh_b[d_][h]], writes=[pb])
            mk = cft[:, CF_MF:CF_MF + 128] if d_ == 0 else cft[:, CF_MB:CF_MB + 128]
            S.op("dve", lambda: V.tensor_tensor(out=scs[:, d_, h, :, :].rearrange("p j c -> p (j c)"), in0=ps[:, 0:128], in1=mk, op=ALU.mult),
                 reads=[B_const], writes=[pb, scs_b[d_][h]])
            done(pb)

        def w_mm(h, d_, m):
            ps, pb = bank()
            j, a = m // 2, m % 2
            S.op("pe", lambda: PE.matmul(ps[:, 0:128], lhsT=khT[a * 64:(a + 1) * 64, d_, h, j, :], rhs=vT[a * 64:(a + 1) * 64, j, h * 128:(h + 1) * 128],
                                         start=True, stop=True), reads=[khT_b[d_][h], vT_b[j]], writes=[pb])
            return ps, pb

        def recur(h, d_, step, m_prev, t0, t0_b, sb_out, sb_out_b, psw, pbw):
            if step == 0:
                tprev, rd, sc = t0, [t0_b], 1.0
            else:
                tprev = Tst[:, d_, h, (step - 1) % 2, :]
                rd = [Tst_b[d_][h][(step - 1) % 2], edg_b[d_][h]]
                sc = edg[:, d_, h, m_prev:m_prev + 1]
            if sb_out is not None:
                S.op("act", lambda: A.activation(out=sb_out, in_=tprev, func=AF.Identity, scale=sc), reads=rd, writes=[sb_out_b])
            S.op("dve", lambda: V.scalar_tensor_tensor(out=Tst[:, d_, h, step % 2, :], in0=tprev, scalar=sc, in1=psw[:, 0:128],
                                                       op0=ALU.mult, op1=ALU.subtract), reads=rd, writes=[pbw, Tst_b[d_][h][step % 2]])
            done(pbw)

        def final_state(h, d_, m_last, dst, dst_b):
            li = (NCHK - 1) % 2
            S.op("act", lambda: A.activation(out=dst, in_=Tst[:, d_, h, li, :], func=AF.Identity, scale=edg[:, d_, h, m_last:m_last + 1]),
                 reads=[Tst_b[d_][h][li], edg_b[d_][h]], writes=[dst_b])

        def pre_block(b, tok0, nxt_tok0):
            pst = load_x(tok0, nxt_tok0)
            evac_x(pst, 0, False)
            wv, wb = w_next("in", 2)
            for h in range(4):
                ps, pb = bank()
                proj_fm(wv, wb, h, ps[:, 0:TB], pb)
                gates_tanh(ps[:, 0:TB], pb, h, 1)
            wv, wb = w_next("in", 3)
            proj_tm(wv, wb, vT, vT_b)
            wv, wb = w_next("in", 5)
            proj_tm(wv, wb, upT, upT_b, eng="dve")
            S.dma("act", ups_sem, ups[tok0:tok0 + TB, :].rearrange("(j p) f -> p j f", p=128), upT, reads=scs_all, writes=[B_ups])
            gates_all([(h, 1) for h in range(4)], False)
            for h in range(4):
                khT_make(h, 1)
            for step, m in enumerate(reversed(range(NCHK))):
                for h in range(4):
                    psw, pbw = w_mm(h, 1, m)
                    recur(h, 1, step, m + 1, carryB[:, h, :], carryB_b[h], None, None, psw, pbw)
            for h in range(4):
                final_state(h, 1, 0, carryB[:, h, :], carryB_b[h])
            if b > 0:
                S.dma("act", bst_sem, bst[b - 1], carryB[:], reads=carryB_b, writes=[B_bst[b - 1]])

        def layer_norm(src_ps_fn, res, res_b, gate_col, outs):
            for j in range(8):
                ps, pb = src_ps_fn(j)
                S.op("dve", lambda ps=ps, j=j: V.scalar_tensor_tensor(out=rr[:, j, :], in0=ps, scalar=gate_col(j), in1=res[:, j, :],
                                                                  op0=ALU.mult, op1=ALU.add), reads=[res_b[j], B_c2], writes=[pb, rr_b[j]])
                done(pb)
                S.op("pool", lambda j=j: G.tensor_copy(out=rbb[:, j, :], in_=rr[:, j, :]), reads=[rr_b[j]], writes=[rbb_b[j]])
                S.op("act", lambda j=j: A.activation(out=rsq[:, j, :], in_=rr[:, j, :], func=AF.Square), reads=[rr_b[j]], writes=[rsq_b[j]])
            pmn, pmnb = bank()
            for j in range(8):
                S.op("pe", lambda j=j: PE.matmul(pmn[:, 0:TB], lhsT=onesb[:, 1, :], rhs=rbb[:, j, :], start=(j == 0), stop=(j == 7)),
                     reads=[rbb_b[j], B_c2], writes=[pmnb])
            pex, pexb = bank()
            for j in range(8):
                S.op("pe", lambda j=j: PE.matmul(pex[:, 0:TB], lhsT=onesb[:, 1, :], rhs=rsq[:, j, :], start=(j == 0), stop=(j == 7)),
                     reads=[rsq_b[j], B_c2], writes=[pexb])
            S.op("act", lambda: A.activation(out=stt[:, 0, :], in_=pmn[:, 0:TB], func=AF.Square), writes=[pmnb, stt_b[0]])
            S.op("dve", lambda: V.tensor_tensor(out=stt[:, 0, :], in0=pex[:, 0:TB], in1=stt[:, 0, :], op=ALU.subtract), writes=[pexb, stt_b[0]])
            done(pexb)
            S.op("act", lambda: A.activation(out=stt[:, 0, :], in_=stt[:, 0, :], func=AF.Ln, bias=epsc[:, 1:2]), reads=[B_c2], writes=[stt_b[0]])
            S.op("act", lambda: A.activation(out=stt[:, 1, :], in_=stt[:, 0, :], func=AF.Exp, scale=-0.5), reads=[stt_b[0]], writes=[stt_b[1]])
            S.op("dve", lambda: V.scalar_tensor_tensor(out=stt[:, 2, :], in0=pmn[:, 0:TB], scalar=-1.0, in1=stt[:, 1, :], op0=ALU.mult, op1=ALU.mult),
                 reads=[stt_b[1]], writes=[pmnb, stt_b[2]])
            done(pmnb)
            for j in range(8):
                S.op("dve", lambda j=j: V.tensor_tensor(out=rr[:, j, :], in0=rr[:, j, :], in1=stt[:, 1, :], op=ALU.mult),
                     reads=[stt_b[1]], writes=[rr_b[j]])
                S.op("pool", lambda j=j: G.tensor_tensor(out=rr[:, j, :], in0=rr[:, j, :], in1=stt[:, 2, :], op=ALU.add),
                     reads=[stt_b[2]], writes=[rr_b[j]])
                for (dst, dstb, scf, bif) in outs:
                    S.op("act", lambda j=j, dst=dst, scf=scf, bif=bif: A.activation(out=dst[:, j, :], in_=rr[:, j, :], func=AF.Identity,
                                                                               scale=scf(j), bias=bif(j)),
                         reads=[rr_b[j], B_c2, B_const], writes=[dstb[j]])

        prefetched = set()

        def fetch_lat(b):
            if ("lat", b) in prefetched:
                return
            prefetched.add(("lat", b))
            if b == NLB - 1:
                S.dma("sp", binit_sem, binit[:], st0[:, 1], writes=[binit_b])
            else:
                S.dma("sp", binit_sem, binit[:], bst[b], reads=[B_bst[b]], writes=[binit_b])
            n_lo, n_hi = max(0, 2 * b - 4), min(32, 2 * b + 6)
            S.dma("sp", uwin_sem, uwin[:, n_lo - (2 * b - 4): n_hi - (2 * b - 4), :],
                  ups[n_lo * 128:n_hi * 128, :].rearrange("(j p) f -> p j f", p=128), reads=[B_ups], writes=[uwin_b])

        def pool_branch(lat, b):
            for g in range(4):
                gi2 = g % 2
                pp, ppb = bank()
                for ml in range(2):
                    if lat:
                        m = 2 * b + ml
                        ins_ = [(_IDX2[(g, dl)], m + dl - (2 * b - 4)) for dl in range(-5, 6) if (g, dl) in _IDX2 and 0 <= m + dl < 32]
                    else:
                        ins_ = [(_IDX1[(g, dl)], ml + dl) for dl in (-1, 0, 1) if (g, dl) in _IDX1 and 0 <= ml + dl < 2]
                    for ii, (pi, ui) in enumerate(ins_):
                        S.op("pe", lambda ml=ml, pi=pi, ui=ui, ii=ii, nn=len(ins_): PE.matmul(
                            pp[:, ml * 128:(ml + 1) * 128], lhsT=ptt[:, pi, :], rhs=uwin[:, ui, g * 128:(g + 1) * 128],
                            start=(ii == 0), stop=(ii == nn - 1)), reads=[uwin_b, B_const], writes=[ppb])
                for ml in range(2):
                    ri = _R2[(g, 2 * b + ml)] if lat else _R1[(g, ml)]
                    ui = (ml + 4) if lat else ml
                    S.op("dve", lambda ml=ml, ri=ri, ui=ui: V.scalar_tensor_tensor(
                        out=dtm[gi2][:, ml, :], in0=pp[:, ml * 128:(ml + 1) * 128], scalar=rct[:, ri:ri + 1], in1=uwin[:, ui, g * 128:(g + 1) * 128],
                        op0=ALU.mult, op1=ALU.subtract), reads=[B_const, uwin_b], writes=[ppb, dtm_b[gi2]])
                done(ppb)
                pt_, ptb = bank()
                ptb16 = pt_[:].bitcast(BF16)
                for ml in range(2):
                    S.op("pe", lambda ml=ml: PE.transpose(ptb16[:, ml * 128:(ml + 1) * 128], dtm[gi2][:, ml, :], identb[:]),
                         reads=[dtm_b[gi2], B_c2], writes=[ptb])
                S.op("act", lambda: A.activation(out=ndf[gi2][:], in_=ptb16[:, 0:TB], func=AF.Copy), writes=[ptb, ndf_b[gi2]])
                done(ptb)
                py, pyb = bank()
                S.op("pe", lambda: PE.matmul(py[:, 0:TB], lhsT=poolwb[:, g, :], rhs=ndf[gi2][:], start=True, stop=True),
                     reads=[ndf_b[gi2], B_c2], writes=[pyb])
                S.op("act", lambda: A.activation(out=ymix[:, 4 + g, :], in_=py[:, 0:TB], func=AF.Identity, scale=nps[:, g:g + 1]),
                     reads=[B_c2], writes=[pyb, ymix_b[4 + g]])
                done(pyb)

        def A_gen(kind, b, tok0, nxt_tok0, nxt):
            lat = kind == "lat"
            w_ = 0 if lat else 1
            if lat:
                fetch_lat(b)
                if b == 0:
                    S.dma("sp", misc_sem, carryF[:], st0[:, 0], writes=carryF_b)
            pst = load_x(tok0, nxt_tok0)
            evac_x(pst, w_, True)
            yield
            wv, wb = w_next("in", 0)
            for h in range(4):
                ps, pb = bank()
                proj_fm(wv, wb, h, ps[:, 0:TB], pb)
                S.op("act", lambda ps=ps, h=h: A.activation(out=sq[h][:], in_=ps[:, 0:TB], func=AF.Silu), writes=[pb, sq_b[h]])
                done(pb)
            yield
            for d_ in range(2):
                wv, wb = w_next("in", 1 + d_)
                for h in range(4):
                    ps, pb = bank()
                    proj_fm(wv, wb, h, ps[:, 0:TB], pb)
                    gates_tanh(ps[:, 0:TB], pb, h, d_)
                yield
            wv, wb = w_next("in", 3)
            proj_tm(wv, wb, vT, vT_b)
            yield
            wv, wb = w_next("in", 4)
            for h in range(4):
                ps, pb = bank()
                proj_fm(wv, wb, h, ps[:, 0:TB], pb)
                S.op("act", lambda ps=ps, h=h: A.activation(out=sgate[:, h, :], in_=ps[:, 0:TB], func=AF.Silu), writes=[pb, sgate_b[h]])
                done(pb)
            if not lat:
                wv, wb = w_next("in", 5)
                proj_tm(wv, wb, uwin, uwin_b)
            yield
            pool_branch(lat, b)
            yield
            gates_all([(h, d_) for d_ in range(2) for h in range(4)], True)
            yield
            for d_ in range(2):
                for h in range(4):
                    khT_make(h, d_)
                    scores(h, d_)
                yield
            ni = b % 2
            for step, m in enumerate(reversed(range(NCHK))):
                for h in range(4):
                    t0, t0_b = (binit[:, h, :], binit_b) if lat else (zst[:], B_c2)
                    psw, pbw = w_mm(h, 1, m)
                    recur(h, 1, step, m + 1, t0, t0_b, Sbb[:, h, m, :], Sbb_b[h][m], psw, pbw)
                if step < NCHK - 1:
                    yield
            if not lat:
                for h in range(4):
                    final_state(h, 1, 0, nsst[ni][:, h, :], nsst_hb(ni, h))
            yield
            pos = [bank() for h in range(4)]
            for m in range(NCHK):
                j, a = m // 2, m % 2
                for h in range(4):
                    po, pob = pos[h]
                    t0, t0_b = (carryF[:, h, :], carryF_b[h]) if lat else (zst[:], B_c2)
                    psw, pbw = w_mm(h, 0, m)
                    recur(h, 0, m, m - 1, t0, t0_b, Sbf[:, h, m % 2, :], Sbf_b[h][m % 2], psw, pbw)
                    oc = po[:, m * 64:(m + 1) * 64]
                    S.op("pe", lambda m=m, oc=oc, h=h: PE.matmul(oc, lhsT=Sbf[:, h, m % 2, :], rhs=qh[:, 0, h, m * 64:(m + 1) * 64], start=True, stop=False),
                         reads=[Sbf_b[h][m % 2], qh_b[0][h]], writes=[pob])
                    S.op("pe", lambda m=m, oc=oc, h=h: PE.matmul(oc, lhsT=vT[a * 64:(a + 1) * 64, j, h * 128:(h + 1) * 128],
                                                               rhs=scs[a * 64:(a + 1) * 64, 0, h, j, :], start=False, stop=False),
                         reads=[vT_b[j], scs_b[0][h]], writes=[pob])
                    S.op("pe", lambda m=m, oc=oc, h=h: PE.matmul(oc, lhsT=Sbb[:, h, m, :], rhs=qh[:, 1, h, m * 64:(m + 1) * 64], start=False, stop=False),
                         reads=[Sbb_b[h][m], qh_b[1][h]], writes=[pob])
                    S.op("pe", lambda m=m, oc=oc, h=h: PE.matmul(oc, lhsT=vT[a * 64:(a + 1) * 64, j, h * 128:(h + 1) * 128],
                                                               rhs=scs[a * 64:(a + 1) * 64, 1, h, j, :], start=False, stop=True),
                         reads=[vT_b[j], scs_b[1][h]], writes=[pob])
                yield
            for h in range(4):
                if h == 2:
                    yield
                po, pob = pos[h]
                final_state(h, 0, NCHK - 1, carryF[:, h, :], carryF_b[h])
                S.op("act", lambda h=h, po=po: A.activation(out=osb[h][:], in_=po[:, 0:TB], func=AF.Copy), writes=[pob, osb_b[h]])
                S.op("act", lambda h=h, po=po: A.activation(out=osq[h % 2][:], in_=po[:, 0:TB], func=AF.Square), writes=[pob, osq_b[h % 2]])
                done(pob)
                pm, pmb = bank()
                S.op("pe", lambda h=h, pm=pm: PE.matmul(pm[:, 0:TB], lhsT=onesb[:, 0, :], rhs=osq[h % 2][:], start=True, stop=True),
                     reads=[osq_b[h % 2], B_c2], writes=[pmb])
                S.op("act", lambda h=h, pm=pm: A.activation(out=rstd_t[:, h, :], in_=pm[:, 0:TB], func=AF.Ln, bias=epsc[:, 0:1]),
                     reads=[B_c2], writes=[pmb, rstd_b])
                done(pmb)
                S.op("dve", lambda h=h: V.scalar_tensor_tensor(out=osb[h][:], in0=osb[h][:], scalar=prmt[:, PRM_GN:PRM_GN + 1], in1=sgate[:, h, :],
                                                           op0=ALU.mult, op1=ALU.mult), reads=[B_const, sgate_b[h]], writes=[osb_b[h]])
            if nxt is not None and nxt[0] == "lat":
                fetch_lat(nxt[1])
            yield
            S.op("act", lambda: A.activation(out=rstd_t[:], in_=rstd_t[:], func=AF.Exp, scale=-0.5), writes=[rstd_b])
            for h in range(4):
                S.op("dve", lambda h=h: V.tensor_tensor(out=ymix[:, h, :], in0=osb[h][:], in1=rstd_t[:, h, :], op=ALU.mult),
                     reads=[osb_b[h], rstd_b], writes=[ymix_b[h]])
            if not lat:
                S.dma("act", nsf_sem, nsf[b].rearrange("h d e -> d h e"), carryF[:], reads=carryF_b)
                S.dma("act", nsb_sems[ni], nsb[b].rearrange("h d e -> d h e"), nsst[ni][:], reads=(carryB_b if ni == 0 else [binit_b]))

        def F_wout_ln1(w_):
            wo = {}

            def wout_ps(j):
                if j // 4 not in wo:
                    wo[j // 4] = w_next("out", j // 4)
                ps, pb = bank()
                wv, wb = wo[j // 4]
                jc = j % 4
                for kc in range(8):
                    S.op("pe", lambda kc=kc: PE.matmul(ps[:, 0:TB], lhsT=wv[:, kc, jc * 128:(jc + 1) * 128], rhs=ymix[:, kc, :],
                                                      start=(kc == 0), stop=(kc == 7)), reads=[wb, ymix_b[kc]], writes=[pb])
                return ps[:, 0:TB], pb

            layer_norm(wout_ps, xTa, xTa_b, lambda j: modv[:, 16 + j, w_:w_ + 1],
                       [(h2, h2_b, lambda j: der[:, w_, 1, j:j + 1], lambda j: der[:, w_, 2, j:j + 1]),
                        (x2a, x2a_b, lambda j: shr[:, 0, j:j + 1], lambda j: shr[:, 1, j:j + 1])])

        gu_open = {}

        def F_gu_jj(u, jl):
            if jl == 0:
                gu_open["g"] = w_next("g", u)
                gu_open["u"] = w_next("u", u)
            wgv, wgb_ = gu_open["g"]
            wuv, wub_ = gu_open["u"]
            jj = u * 4 + jl
            ps, pb = bank()
            for kc in range(8):
                S.op("pe", lambda kc=kc: PE.matmul(ps[:, 0:TB], lhsT=wgv[:, kc, jl * 128:(jl + 1) * 128], rhs=h2[:, kc, :],
                                                  start=(kc == 0), stop=(kc == 7)), reads=[wgb_, h2_b[kc]], writes=[pb])
            ps2, pb2 = bank()
            for kc in range(8):
                S.op("pe", lambda kc=kc: PE.matmul(ps2[:, 0:TB], lhsT=wuv[:, kc, jl * 128:(jl + 1) * 128], rhs=h2[:, kc, :],
                                                  start=(kc == 0), stop=(kc == 7)), reads=[wub_, h2_b[kc]], writes=[pb2])
            gi3 = jj % 2
            S.op("act", lambda: A.activation(out=sgt[gi3][:], in_=ps[:, 0:TB], func=AF.Silu), writes=[pb, sgt_b[gi3]])
            done(pb)
            S.op("dve", lambda: V.tensor_tensor(out=act[:, jj, :], in0=ps2[:, 0:TB], in1=sgt[gi3][:], op=ALU.mult),
                 reads=[sgt_b[gi3]], writes=[pb2, act_b[jj]] + stg_b)
            done(pb2)

        def F_dn_ln2(w_, hook):
            dn_cache = {}

            def down_ps(j):
                if j > 0:
                    hook()
                jp, jl = j // 2, j % 2
                if jp not in dn_cache:
                    dn_cache.clear()
                    dn_cache[jp] = [w_next("d", jp, 0), w_next("d", jp, 1)]
                ps, pb = bank()
                for hf in range(2):
                    wv, wb = dn_cache[jp][hf]
                    for kl in range(11):
                        kk = hf * 11 + kl
                        S.op("pe", lambda kl=kl, kk=kk, wv=wv: PE.matmul(ps[:, 0:TB], lhsT=wv[:, kl, jl * 128:(jl + 1) * 128], rhs=act[:, kk, :],
                                                                       start=(kk == 0), stop=(kk == 21)), reads=[wb, act_b[kk]], writes=[pb])
                return ps[:, 0:TB], pb

            layer_norm(down_ps, x2a, x2a_b, lambda j: modv[:, 40 + j, w_:w_ + 1],
                       [(outF, outF_b, lambda j: prmt[:, PRM_L2G + j:PRM_L2G + j + 1], lambda j: prmt[:, PRM_L2B + j:PRM_L2B + j + 1])])

        def F_out(tok0):
            for j in range(2):
                for half in range(2):
                    ps, pb = bank()
                    for kl in range(4):
                        kc = half * 4 + kl
                        S.op("pe", lambda kl=kl, kc=kc: PE.transpose(ps[:, kl * 128:(kl + 1) * 128], outF[:, kc, j * 128:(j + 1) * 128],
                                                                    cft[:, CF_ID:CF_ID + 128]), reads=[outF_b[kc], B_const], writes=[pb])
                    S.op("act", lambda: A.activation(out=stg[j][:, half * 512:(half + 1) * 512], in_=ps[:, :], func=AF.Copy),
                         writes=[pb, stg_b[j]] + act_b)
                    done(pb)
                S.dma("act", stg_sem[j], y[tok0 + j * 128: tok0 + (j + 1) * 128, :], stg[j][:], reads=[stg_b[j]])

        S.dma("sp", misc_sem, carryB[:], st0[:, 1], writes=carryB_b)
        w_issue()

        def tok_of(kind, b):
            return b * TB if kind in ("pre", "lat") else LAT_T + b * PR_T

        issue_x(tok_of(*plan[0]))
        pre = [p for p in plan if p[0] == "pre"]
        mains = [p for p in plan if p[0] != "pre"]
        for pi_, (kind, b) in enumerate(pre):
            nxt = plan[pi_ + 1]
            pre_block(b, tok_of(kind, b), tok_of(*nxt))

        def mk_A(i):
            kind, b = mains[i]
            nxt = mains[i + 1] if i + 1 < len(mains) else None
            return A_gen(kind, b, tok_of(kind, b), tok_of(*nxt) if nxt is not None else None, nxt)

        def drain(gen, n=None):
            k = 0
            while gen is not None and (n is None or k < n):
                try:
                    next(gen)
                except StopIteration:
                    return None
                k += 1
            return gen

        drain(mk_A(0))
        for i, (kind, b) in enumerate(mains):
            w_ = 0 if kind == "lat" else 1
            F_wout_ln1(w_)
            if i > 0:
                F_out(tok_of(*mains[i - 1]))
            hold = [mk_A(i + 1) if i + 1 < len(mains) else None]

            def adv(n=1):
                hold[0] = drain(hold[0], n)

            adv(2)
            for u in range(6):
                for jl in range(4 if u < 5 else 2):
                    F_gu_jj(u, jl)
                    if u >= 2:
                        adv()
                if u < 2:
                    adv(2)
            F_dn_ln2(w_, adv)
            drain(hold[0])
        F_out(tok_of(*mains[-1]))
        for s_ in stg_sem + [nsf_sem] + nsb_sems:
            SP.wait_ge(S.sems[s_], S.cnt[s_])
        build_program.stats = (dict(S.cnt), S.nwait)
        build_program.order = list(wq)
    return nc


_NC_CACHE = {}


def _prep_inputs(inp):
    f = lambda a: np.ascontiguousarray(np.asarray(a, dtype=np.float32))
    xp, xsm = f(inp["x_prompt"]), f(inp["x_sample"])
    sf, sbk = f(inp["state_hgrn_fwd"]), f(inp["state_hgrn_bwd"])
    c, cctx = f(inp["c"]), f(inp["c_ctx"])
    shared = {
        "w_mod": f(inp["w_mod"][0]),
        "b_mod": np.ascontiguousarray(f(inp["b_mod"][0]).reshape(48, 128).T),
        "w_in": f(inp["w_in"][0]),
        "w_out": f(inp["w_out"][0]),
        "wg": f(inp["w_ffn_gate"][0]),
        "wu": f(inp["w_ffn_up"][0]),
        "wd": f(inp["w_ffn_down"][0]),
        "poolw": np.ascontiguousarray(f(inp["pool_w"][0]).transpose(1, 0, 2)),
        "cf": _CF,
        "ptab": _PT,
        "rtab": _RC,
    }
    prm = np.zeros((128, PRM_N), np.float32)
    lf, lb_ = f(inp["lb_fwd_raw"]), f(inp["lb_bwd_raw"])
    prm[:, PRM_LBF:PRM_LBF + 8] = lf.reshape(2, 4, 128).transpose(2, 1, 0).reshape(128, 8)
    prm[:, PRM_LBB:PRM_LBB + 8] = lb_.reshape(2, 4, 128).transpose(2, 1, 0).reshape(128, 8)
    prm[:, PRM_GN] = f(inp["hgrn_norm_g"][0])
    prm[:, PRM_PS:PRM_PS + 4] = f(inp["pool_scale"][0]).reshape(4, 128).T
    prm[:, PRM_L1G:PRM_L1G + 8] = f(inp["ln1_g"][0]).reshape(8, 128).T
    prm[:, PRM_L1B:PRM_L1B + 8] = f(inp["ln1_b"][0]).reshape(8, 128).T
    prm[:, PRM_L2G:PRM_L2G + 8] = f(inp["ln2_g"][0]).reshape(8, 128).T
    prm[:, PRM_L2B:PRM_L2B + 8] = f(inp["ln2_b"][0]).reshape(8, 128).T
    shared["prm"] = prm
    maps = []
    for i in range(NCORES):
        m = dict(shared)
        m["xs"] = np.concatenate([xsm[i], xp[i * NPR:(i + 1) * NPR].reshape(NPR * PR_T, D)], 0)
        st = np.stack([sf[i, 0], sbk[i, 0]], 0)
        m["st0"] = np.ascontiguousarray(st.transpose(2, 0, 1, 3))
        cv = np.stack([c[i], cctx], -1)
        m["cvec"] = np.ascontiguousarray(cv.reshape(8, 128, 2).transpose(1, 0, 2))
        maps.append(m)
    return maps


def kernel(**inputs):
    if "nc" not in _NC_CACHE:
        build_program()
        _NC_CACHE["nc"] = build_program(build_program.order)
    nc = _NC_CACHE["nc"]
    maps = _prep_inputs(inputs)
    res = run_bass_kernel_spmd(nc, maps, core_ids=list(range(NCORES)))
    ys = [np.asarray(r["y"], dtype=np.float32) for r in res.results]
    y_sample = np.stack([a[:LAT_T] for a in ys], 0)
    y_prompt = np.concatenate([a[LAT_T:].reshape(NPR, PR_T, D) for a in ys], 0)
    nsf = np.concatenate([np.asarray(r["nsf"], dtype=np.float32) for r in res.results], 0)[:, None]
    nsb = np.concatenate([np.asarray(r["nsb"], dtype=np.float32) for r in res.results], 0)[:, None]
    return (y_prompt, y_sample, nsf, nsb)
```

```python
import numpy as np
import ml_dtypes
from contextlib import ExitStack
import concourse.bass as bass
import concourse.mybir as mybir
from concourse.bass_utils import run_bass_kernel_spmd

F32 = mybir.dt.float32
BF16 = mybir.dt.bfloat16
AF = mybir.ActivationFunctionType
ALU = mybir.AluOpType

NCORES = 8
D = 1024
DFF = 2816
TB = 256
CH = 64
NCHK = TB // CH
LAT_T = 4096
PR_T = 256
NPR = 4
TOK = LAT_T + NPR * PR_T
ALPHA = 2.0 ** 0.25
LN_EPS = 1e-5
RMS_EPS = 1e-6
WINS = (2, 4, 8, 16)
NSLOT = 4


def _pool_tables():
    P = []
    idx2 = {}
    idx1 = {}
    for g, w in enumerate(WINS):
        h = w // 2
        for dl in range(-5, 6):
            m = np.zeros((128, 128), np.float32)
            for a in range(2):
                for a2 in range(2):
                    if a2 - h <= 2 * dl + a < a2 + h:
                        for c2 in range(64):
                            lo, hi = max(c2 - h, 0), min(c2 + h, 64)
                            m[a * 64 + lo:a * 64 + hi, a2 * 64 + c2] = 1.0
            if m.any():
                idx2[(g, dl)] = len(P)
                P.append(m)
        for dl in (-1, 0, 1):
            m = np.zeros((128, 128), np.float32)
            for i2 in range(128):
                lo, hi = i2 - h - 128 * dl, i2 + h - 128 * dl
                lo, hi = max(lo, 0), min(hi, 128)
                if hi > lo:
                    m[lo:hi, i2] = 1.0
            if m.any():
                idx1[(g, dl)] = len(P)
                P.append(m)
    R = []
    rkey = {}
    r2 = {}
    r1 = {}

    def add(v):
        k = v.tobytes()
        if k not in rkey:
            rkey[k] = len(R)
            R.append(v)
        return rkey[k]

    for g, w in enumerate(WINS):
        h = w // 2
        cc = np.array([min(c + h, 64) - max(c - h, 0) for c in range(64)], np.float32)
        for m in range(32):
            v = np.zeros(128, np.float32)
            for a in range(2):
                r = 2 * m + a
                rc = min(r + h, 64) - max(r - h, 0)
                v[a * 64:(a + 1) * 64] = 1.0 / (rc * cc)
            r2[(g, m)] = add(v)
        for m in range(2):
            t = np.arange(128) + 128 * m
            cnt = np.minimum(t + h, 256) - np.maximum(t - h, 0)
            r1[(g, m)] = add((1.0 / cnt).astype(np.float32))
    PT = np.stack(P, 1)
    RC = np.ascontiguousarray(np.stack(R, 1))
    return PT.astype(ml_dtypes.bfloat16), RC.astype(np.float32), idx2, idx1, r2, r1


_PT, _RC, _IDX2, _IDX1, _R2, _R1 = _pool_tables()
NPT = _PT.shape[1]
NRC = _RC.shape[1]


def _const_f32():
    ident = np.eye(128, dtype=np.float32)
    p = np.arange(128) % 64
    c = np.arange(64)
    mF = np.where(c[None, :] >= p[:, None], -1.0, 0.0).astype(np.float32)
    mB = np.where(c[None, :] <= p[:, None], -1.0, 0.0).astype(np.float32)
    mF = np.repeat(mF[:, None, :], 2, 1).reshape(128, 128)
    mB = np.repeat(mB[:, None, :], 2, 1).reshape(128, 128)
    st = np.zeros((128, TB), np.float32)
    st[:, ::CH] = 1.0
    en = np.zeros((128, TB), np.float32)
    en[:, CH - 1::CH] = 1.0
    return np.concatenate([ident, mF, mB, st, en, 1.0 - st, 1.0 - en], 1)


_CF = _const_f32()
CF_ID, CF_MF, CF_MB, CF_ST, CF_EN, CF_NST, CF_NEN = 0, 128, 256, 384, 384 + TB, 384 + 2 * TB, 384 + 3 * TB
PRM_LBF, PRM_LBB, PRM_GN, PRM_PS, PRM_L1G, PRM_L1B, PRM_L2G, PRM_L2B, PRM_N = 0, 8, 16, 17, 21, 29, 37, 45, 53


class Buf:
    __slots__ = ("w", "r", "name")

    def __init__(self, name=""):
        self.w = None
        self.r = {}
        self.name = name


class Sched:
    def __init__(self, nc, es):
        self.nc = nc
        self.eng = {"pe": nc.tensor, "act": nc.scalar, "dve": nc.vector, "pool": nc.gpsimd, "sp": nc.sync}
        self.sems = {}
        self.cnt = {}
        self.waited = {e: {} for e in self.eng}
        self.es = es
        for e in self.eng:
            self.sems[e] = es.enter_context(nc.semaphore("prog_" + e))
            self.cnt[e] = 0
        self.nwait = 0

    def new_sem(self, name):
        s = self.es.enter_context(self.nc.semaphore(name))
        self.sems[name] = s
        self.cnt[name] = 0
        return name

    def _deps(self, e, reads, writes):
        deps = {}

        def add(tok):
            if tok is None:
                return
            k, v = tok
            if k == e and e == "pe":
                return
            if deps.get(k, 0) < v:
                deps[k] = v

        for b in reads:
            add(b.w)
        for b in writes:
            add(b.w)
            for k, v in b.r.items():
                add((k, v))
        out = []
        for k, v in deps.items():
            if self.waited[e].get(k, 0) >= v:
                continue
            self.waited[e][k] = v
            out.append((k, v))
        return out

    def _emit(self, e, fn, deps):
        eng = self.eng[e]
        for k, v in deps[1:]:
            eng.wait_ge(self.sems[k], v)
            self.nwait += 1
        ins = fn()
        if deps:
            ins._wait_ge(self.sems[deps[0][0]], deps[0][1])
        return ins

    def op(self, e, fn, reads=(), writes=()):
        deps = self._deps(e, reads, writes)
        ins = self._emit(e, fn, deps)
        self.cnt[e] += 1
        ins.then_inc(self.sems[e], 1)
        tok = (e, self.cnt[e])
        for b in reads:
            if b.r.get(e, 0) < tok[1]:
                b.r[e] = tok[1]
        for b in writes:
            b.w = tok
            b.r = {}
        return ins

    def dma(self, q, sem, out, in_, reads=(), writes=(), **kw):
        deps = self._deps(q, reads, writes)
        eng = self.eng[q]
        ins = self._emit(q, lambda: eng.dma_start(out=out, in_=in_, **kw), deps)
        self.cnt[sem] += 16
        ins.then_inc(self.sems[sem], 16)
        tok = (sem, self.cnt[sem])
        for b in reads:
            if b.r.get(sem, 0) < tok[1]:
                b.r[sem] = tok[1]
        for b in writes:
            b.w = tok
            b.r = {}
        return ins


def build_program(order=None):
    nc = bass.Bass("TRN2", target_bir_lowering=False)

    def din(name, shape, dt=F32):
        return nc.dram_tensor(name, list(shape), dt, kind="ExternalInput").ap()

    xs = din("xs", [TOK, D])
    st0 = din("st0", [128, 2, 4, 128])
    cvec = din("cvec", [128, 8, 2])
    w_mod = din("w_mod", [D, 6 * D])
    b_mod = din("b_mod", [128, 48])
    w_in = din("w_in", [D, 3 * D])
    w_out = din("w_out", [D, D])
    wg = din("wg", [D, DFF])
    wu = din("wu", [D, DFF])
    wd = din("wd", [DFF, D])
    poolw = din("poolw", [128, 4, 128])
    prm = din("prm", [128, PRM_N])
    cf = din("cf", list(_CF.shape))
    ptab = din("ptab", [128, NPT, 128], BF16)
    rtab = din("rtab", [128, NRC])
    y = nc.dram_tensor("y", [TOK, D], F32, kind="ExternalOutput").ap()
    nsf = nc.dram_tensor("nsf", [NPR, 4, 128, 128], F32, kind="ExternalOutput").ap()
    nsb = nc.dram_tensor("nsb", [NPR, 4, 128, 128], F32, kind="ExternalOutput").ap()
    w_in_b = nc.dram_tensor("w_in_b", [D, 3 * D], BF16).ap()
    w_out_b = nc.dram_tensor("w_out_b", [D, D], BF16).ap()
    wg_b = nc.dram_tensor("wg_b", [D, DFF], BF16).ap()
    wu_b = nc.dram_tensor("wu_b", [D, DFF], BF16).ap()
    wd_b = nc.dram_tensor("wd_b", [DFF, D], BF16).ap()
    ups = nc.dram_tensor("ups", [LAT_T, 512], BF16).ap()
    bst = nc.dram_tensor("bst", [LAT_T // TB, 128, 4, 128], F32).ap()

    es = ExitStack()
    with es:
        S = Sched(nc, es)

        def sb(name, shape, dt=F32):
            return es.enter_context(nc.sbuf_tensor(name, list(shape), dt))

        def bufs(n, name):
            return [Buf(f"{name}{i}") for i in range(n)]

        cft = sb("cft", _CF.shape)
        identb = sb("identb", [128, 128], BF16)
        onesb = sb("onesb", [128, 2, 128], BF16)
        ptt = sb("ptt", [128, NPT, 128], BF16)
        rct = sb("rct", [128, NRC])
        lnc1 = sb("lnc1", [128, 2, 4])
        epsc = sb("epsc", [128, 2])
        prmt = sb("prmt", [128, PRM_N])
        poolwf = sb("poolwf", [128, 4, 128])
        poolwb = sb("poolwb", [128, 4, 128], BF16)
        cvt = sb("cvt", [128, 16])
        scv = sb("scv", [128, 16])
        bmt = sb("bmt", [128, 48])
        modv = sb("modv", [128, 48, 2])
        der = sb("der", [128, 2, 3, 8])
        shr = sb("shr", [128, 2, 8])
        lbc = sb("lbc", [128, 2, 2, 4])
        lbt = sb("lbt", [128, 2, 4])
        nps = sb("nps", [128, 4])
        B_const = Buf("const")

        slots = [sb(f"wslot{i}", [128, 2048]) for i in range(NSLOT)]
        slot_b = bufs(NSLOT, "wslot")
        slot_sem = [S.new_sem(f"wsem{i}") for i in range(NSLOT)]

        xin = [sb(f"xin{i}", [128, D]) for i in range(2)]
        xin_b = bufs(2, "xin")
        xin_sem = [S.new_sem(f"xsem{i}") for i in range(2)]
        xTa = sb("xTa", [128, 8, TB])
        xTa_b = bufs(8, "xTa")
        hT = sb("hT", [128, 8, TB], BF16)
        hT_b = bufs(8, "hT")
        tb = sb("tb", [128, 8, TB])
        tb_b = bufs(8, "tb")
        gw = sb("gw", [128, 6, TB])
        gw_b = bufs(6, "gw")
        sq = [sb(f"sq{i}", [128, TB]) for i in range(4)]
        sq_b = bufs(4, "sq")
        edg = sb("edg", [128, 2, 4, NCHK])
        edg_b = [bufs(4, f"edg{d}_") for d in range(2)]
        qh = sb("qh", [128, 2, 4, TB], BF16)
        qh_b = [bufs(4, f"qh{d}_") for d in range(2)]
        kh = sb("kh", [128, 2, 4, TB], BF16)
        kh_b = [bufs(4, f"kh{d}_") for d in range(2)]
        khT = sb("khT", [128, 2, 4, 2, 128], BF16)
        khT_b = [bufs(4, f"khT{d}_") for d in range(2)]
        vT = sb("vT", [128, 2, 512], BF16)
        vT_b = bufs(2, "vT")
        sgate = sb("sgate", [128, 4, TB], BF16)
        sgate_b = bufs(4, "sgate")
        scs = sb("scs", [128, 2, 4, 2, CH], BF16)
        scs_b = [bufs(4, f"scs{d}_") for d in range(2)]
        Tst = sb("Tst", [128, 2, 4, 2, 128])
        Tst_b = [[bufs(2, f"T{d}{h}_") for h in range(4)] for d in range(2)]
        Sbb = sb("Sbb", [128, 4, NCHK, 128], BF16)
        Sbb_b = [bufs(NCHK, f"Sbb{h}_") for h in range(4)]
        Sbf = sb("Sbf", [128, 4, 2, 128], BF16)
        Sbf_b = [bufs(2, f"Sbf{h}_") for h in range(4)]
        carryF = sb("carryF", [128, 4, 128])
        carryF_b = bufs(4, "carryF")
        carryB = sb("carryB", [128, 4, 128])
        carryB_b = bufs(4, "carryB")
        binit = sb("binit", [128, 4, 128])
        binit_b = Buf("binit")
        binit_sem = S.new_sem("binit_sem")
        zst = sb("zst", [128, 128])
        osb = [sb(f"osb{i}", [128, TB]) for i in range(4)]
        osb_b = bufs(4, "osb")
        osq = [sb(f"osq{i}", [128, TB], BF16) for i in range(2)]
        osq_b = bufs(2, "osq")
        rstd_t = sb("rstd_t", [128, 4, TB])
        rstd_b = Buf("rstd")
        ymix = sb("ymix", [128, 8, TB], BF16)
        ymix_b = bufs(8, "ymix")
        uwin = sb("uwin", [128, 10, 512], BF16)
        uwin_b = Buf("uwin")
        uwin_sem = S.new_sem("uwin_sem")
        dtm = [sb(f"dtm{i}", [128, 2, 128], BF16) for i in range(2)]
        dtm_b = bufs(2, "dtm")
        ndf = [sb(f"ndf{i}", [128, TB], BF16) for i in range(2)]
        ndf_b = bufs(2, "ndf")
        rr = sb("rr", [128, 8, TB])
        rr_b = bufs(8, "rr")
        rbb = sb("rbb", [128, 8, TB], BF16)
        rbb_b = bufs(8, "rbb")
        rsq = sb("rsq", [128, 8, TB], BF16)
        rsq_b = bufs(8, "rsq")
        stt = sb("stt", [128, 4, TB])
        stt_b = bufs(4, "stt")
        x2a = sb("x2a", [128, 8, TB])
        x2a_b = bufs(8, "x2a")
        h2 = sb("h2", [128, 8, TB], BF16)
        h2_b = bufs(8, "h2")
        act = sb("act", [128, 22, TB], BF16)
        act_b = bufs(22, "act")
        sgt = [sb(f"sgt{i}", [128, TB]) for i in range(2)]
        sgt_b = bufs(2, "sgt")
        outF = sb("outF", [128, 8, TB])
        outF_b = bufs(8, "outF")
        actf = act[:].rearrange("p a b -> p (a b)").bitcast(F32)
        stg = [actf[:, i * D:(i + 1) * D] for i in range(2)]
        stg_b = bufs(2, "stg")
        stg_sem = [S.new_sem(f"stgsem{i}") for i in range(2)]
        misc_sem = S.new_sem("misc_sem")
        ups_sem = S.new_sem("ups_sem")
        bst_sem = S.new_sem("bst_sem")
        nsf_sem = S.new_sem("nsf_sem")
        nsb_sems = [S.new_sem(f"nsb_sem{i}") for i in range(2)]
        cast_sems = {k: S.new_sem("cast_" + k) for k in ("in", "out", "g", "u", "d")}
        B_ups = Buf("ups_dram")
        B_bst = [Buf(f"bst{i}") for i in range(LAT_T // TB)]
        B_wb = {k: Buf("wb_" + k) for k in ("in", "out", "g", "u", "d")}
        upT = scs[:].rearrange("p a b c d -> p (a b c d)").rearrange("p (j f) -> p j f", f=512)
        scs_all = scs_b[0] + scs_b[1]
        upT_b = [scs_all, scs_all]
        nsst = [carryB, binit]

        def nsst_hb(ni, h):
            return carryB_b[h] if ni == 0 else binit_b

        psum = [es.enter_context(nc.psum_tensor(f"ps{i}", [128, 512], F32)) for i in range(8)]
        ps_b = bufs(8, "ps")
        pctr = [0]

        live = set()

        def bank():
            for _ in range(8):
                i = pctr[0] % 8
                pctr[0] += 1
                if i not in live:
                    live.add(i)
                    return psum[i], ps_b[i]
            raise RuntimeError("no free PSUM bank")

        def done(pb):
            live.discard(ps_b.index(pb))

        block = es.enter_context(nc.Block())
        V, A, G, PE, SP = nc.vector, nc.scalar, nc.gpsimd, nc.tensor, nc.sync

        def cast(dst, src, rows, cols, key):
            for c0 in range(0, cols, 1024):
                c1 = min(cols, c0 + 1024)
                for r0 in range(0, rows, 512):
                    r1 = min(rows, r0 + 512)
                    S.dma("pool", cast_sems[key], dst[r0:r1, c0:c1], src[r0:r1, c0:c1], writes=[B_wb[key]])

        cast(w_in_b, w_in, D, 3 * D, "in")

        S.dma("sp", misc_sem, cft[:], cf[:], writes=[B_const])
        S.dma("sp", misc_sem, ptt[:], ptab[:], writes=[B_const])
        S.dma("sp", misc_sem, rct[:], rtab[:], writes=[B_const])
        S.dma("sp", misc_sem, prmt[:], prm[:], writes=[B_const])
        S.dma("sp", misc_sem, poolwf[:], poolw[:], writes=[B_const])
        S.dma("sp", misc_sem, cvt[:], cvec.rearrange("p a b -> p (a b)"), writes=[B_const])
        S.dma("sp", misc_sem, bmt[:], b_mod[:], writes=[B_const])
        B_c2 = Buf("const2")
        S.op("dve", lambda: V.tensor_copy(out=identb[:], in_=cft[:, CF_ID:CF_ID + 128]), reads=[B_const], writes=[B_c2])
        S.op("dve", lambda: V.memset(onesb[:, 0, :], 1.0 / 128.0), writes=[B_c2])
        S.op("dve", lambda: V.memset(onesb[:, 1, :], 1.0 / 1024.0), writes=[B_c2])
        S.op("dve", lambda: V.memset(zst[:], 0.0), writes=[B_c2])
        S.op("dve", lambda: V.tensor_copy(out=poolwb[:], in_=poolwf[:]), reads=[B_const], writes=[B_c2])
        S.op("act", lambda: A.activation(out=scv[:], in_=cvt[:], func=AF.Silu), reads=[B_const], writes=[B_c2])
        for d_, col in ((0, PRM_LBF), (1, PRM_LBB)):
            rv = prmt[:, col:col + 8].rearrange("p (h r) -> p h r", r=2)
            S.op("dve", lambda rv=rv, d_=d_: V.tensor_tensor(out=lbt[:, d_, :], in0=rv[:, :, 0], in1=rv[:, :, 1], op=ALU.subtract),
                 reads=[B_const], writes=[B_c2])
            S.op("act", lambda d_=d_: A.activation(out=lbt[:, d_, :], in_=lbt[:, d_, :], func=AF.Tanh, scale=0.5),
                 reads=[B_c2], writes=[B_c2])
            S.op("dve", lambda d_=d_: V.tensor_scalar(out=lbc[:, d_, 0, :], in0=lbt[:, d_, :], scalar1=-0.25, scalar2=0.25,
                                                    op0=ALU.mult, op1=ALU.add), reads=[B_c2], writes=[B_c2])
            S.op("dve", lambda d_=d_: V.tensor_scalar(out=lbc[:, d_, 1, :], in0=lbt[:, d_, :], scalar1=0.25, scalar2=0.75,
                                                    op0=ALU.mult, op1=ALU.add), reads=[B_c2], writes=[B_c2])
        for d_ in range(2):
            S.op("act", lambda d_=d_: A.activation(out=lnc1[:, d_, :], in_=lbc[:, d_, 0, :], func=AF.Ln), reads=[B_c2], writes=[B_c2])
        S.op("dve", lambda: V.memset(epsc[:, 0:1], RMS_EPS), writes=[B_c2])
        S.op("dve", lambda: V.memset(epsc[:, 1:2], LN_EPS), writes=[B_c2])
        S.op("dve", lambda: V.tensor_copy(out=nps[:], in_=prmt[:, PRM_PS:PRM_PS + 4]), reads=[B_const], writes=[B_c2])
        S.op("dve", lambda: V.tensor_scalar(out=shr[:, 0, :], in0=prmt[:, PRM_L1G:PRM_L1G + 8], scalar1=ALPHA, scalar2=None, op0=ALU.mult),
             reads=[B_const], writes=[B_c2])
        S.op("dve", lambda: V.tensor_scalar(out=shr[:, 1, :], in0=prmt[:, PRM_L1B:PRM_L1B + 8], scalar1=ALPHA, scalar2=None, op0=ALU.mult),
             reads=[B_const], writes=[B_c2])

        mps, mps_b = bank()
        scv3 = scv[:].rearrange("p (a b) -> p a b", b=2)
        def mod_load(u):
            si = u % NSLOT
            S.dma("sp", slot_sem[si], slots[si][:].rearrange("p (k c) -> p k c", c=256),
                  w_mod[:, u * 256:(u + 1) * 256].rearrange("(k p) c -> p k c", p=128), writes=[slot_b[si]])

        for u in range(3):
            mod_load(u)
        for u in range(24):
            si = u % NSLOT
            if u + 3 < 24:
                mod_load(u + 3)
            wv = slots[si][:].rearrange("p (k c) -> p k c", c=256)
            for jj in range(2):
                j = u * 2 + jj
                for kc in range(8):
                    S.op("pe", lambda j=j, kc=kc, jj=jj, wv=wv: PE.matmul(
                        mps[:, j * 2:(j + 1) * 2], lhsT=wv[:, kc, jj * 128:(jj + 1) * 128], rhs=scv3[:, kc, :],
                        start=(kc == 0), stop=(kc == 7)), reads=[slot_b[si], B_c2], writes=[mps_b])
        mp3 = mps[:, 0:96].rearrange("p (j w) -> p j w", w=2)
        for w_ in range(2):
            S.op("dve", lambda w_=w_: V.tensor_tensor(out=modv[:, :, w_], in0=mp3[:, :, w_], in1=bmt[:], op=ALU.add),
                 reads=[B_const], writes=[mps_b, B_c2])
            if w_ == 1:
                done(mps_b)
            S.op("dve", lambda w_=w_: V.tensor_scalar(out=der[:, w_, 0, :], in0=modv[:, 8:16, w_], scalar1=1.0, scalar2=None, op0=ALU.add),
                 reads=[B_c2], writes=[B_c2])
            S.op("dve", lambda w_=w_: V.tensor_scalar(out=der[:, w_, 2, :], in0=modv[:, 32:40, w_], scalar1=1.0, scalar2=None, op0=ALU.add),
                 reads=[B_c2], writes=[B_c2])
            S.op("dve", lambda w_=w_: V.tensor_tensor(out=der[:, w_, 1, :], in0=der[:, w_, 2, :], in1=prmt[:, PRM_L1G:PRM_L1G + 8], op=ALU.mult),
                 reads=[B_c2], writes=[B_c2])
            S.op("dve", lambda w_=w_: V.tensor_tensor(out=der[:, w_, 2, :], in0=der[:, w_, 2, :], in1=prmt[:, PRM_L1B:PRM_L1B + 8], op=ALU.mult),
                 reads=[B_c2], writes=[B_c2])
            S.op("dve", lambda w_=w_: V.tensor_tensor(out=der[:, w_, 2, :], in0=der[:, w_, 2, :], in1=modv[:, 24:32, w_], op=ALU.add),
                 reads=[B_c2], writes=[B_c2])

        cast(w_out_b, w_out, D, D, "out")
        cast(wg_b, wg, D, DFF, "g")
        cast(wu_b, wu, D, DFF, "u")
        cast(wd_b, wd, DFF, D, "d")

        def spec_of(ident):
            kind_ = ident[0]
            if kind_ == "in":
                u = ident[1]
                return (w_in_b[:, u * 512:(u + 1) * 512].rearrange("(k p) c -> p k c", p=128), (8, 512), "in")
            if kind_ == "out":
                u = ident[1]
                return (w_out_b[:, u * 512:(u + 1) * 512].rearrange("(k p) c -> p k c", p=128), (8, 512), "out")
            if kind_ in ("g", "u"):
                u = ident[1]
                wb_ = wg_b if kind_ == "g" else wu_b
                c0 = u * 512
                c1 = min(DFF, c0 + 512)
                return (wb_[:, c0:c1].rearrange("(k p) c -> p k c", p=128), (8, c1 - c0), kind_)
            jp, hf = ident[1], ident[2]
            return (wd_b[hf * 1408:(hf + 1) * 1408, jp * 256:(jp + 1) * 256].rearrange("(k p) c -> p k c", p=128), (11, 256), "d")

        recording = order is None
        wq = [] if recording else list(order)
        wstate = {"issued": 0, "used": 0}

        def w_issue():
            while wstate["issued"] < len(wq) and wstate["issued"] < max(wstate["used"] - 1, 0) + NSLOT - 1:
                i = wstate["issued"]
                src_, (kk, cc), key = spec_of(wq[i])
                si = i % NSLOT
                dst = slots[si][:].bitcast(BF16)[:, 0:kk * cc].rearrange("p (k c) -> p k c", c=cc)
                S.dma("sp", slot_sem[si], dst, src_, reads=[B_wb[key]], writes=[slot_b[si]])
                wstate["issued"] += 1

        def w_next(*ident):
            i = wstate["used"]
            if recording:
                wq.append(ident)
            assert tuple(wq[i]) == tuple(ident), (i, wq[i], ident)
            src_, (kk, cc), key = spec_of(wq[i])
            wstate["used"] += 1
            w_issue()
            assert wstate["issued"] > i
            si = i % NSLOT
            return slots[si][:].bitcast(BF16)[:, 0:kk * cc].rearrange("p (k c) -> p k c", c=cc), slot_b[si]

        NLB = LAT_T // TB
        plan = []
        for b in reversed(range(NLB)):
            plan.append(("pre", b))
        for b in range(NLB):
            plan.append(("lat", b))
        for s_ in range(NPR):
            plan.append(("pr", s_))

        def issue_x(tok0):
            for j in range(2):
                S.dma("sp", xin_sem[j], xin[j][:], xs[tok0 + j * 128: tok0 + (j + 1) * 128, :], writes=[xin_b[j]])

        def load_x(tok0, nxt_tok0):
            pst = []
            for kp in range(4):
                ps, pb = bank()
                for kk in range(2):
                    kc = 2 * kp + kk
                    for j in range(2):
                        S.op("pe", lambda ps=ps, kk=kk, j=j, kc=kc: PE.transpose(
                            ps[:, kk * 256 + j * 128: kk * 256 + (j + 1) * 128], xin[j][:, kc * 128:(kc + 1) * 128],
                            cft[:, CF_ID:CF_ID + 128]), reads=[xin_b[j], B_const], writes=[pb])
                pst.append((ps, pb))
            if nxt_tok0 is not None:
                issue_x(nxt_tok0)
            return pst

        def evac_x(pst, w_, want_xta):
            for kp in range(4):
                ps, pb = pst[kp]
                for kk in range(2):
                    kc = 2 * kp + kk
                    if want_xta or kc % 2 == 0:
                        S.op("act", lambda ps=ps, kk=kk, kc=kc: A.activation(
                            out=hT[:, kc, :], in_=ps[:, kk * 256:(kk + 1) * 256], func=AF.Identity,
                            scale=der[:, w_, 0, kc:kc + 1], bias=modv[:, kc, w_:w_ + 1]), reads=[B_c2], writes=[pb, hT_b[kc]])
                    else:
                        S.op("dve", lambda ps=ps, kk=kk, kc=kc: V.tensor_scalar(
                            out=hT[:, kc, :], in0=ps[:, kk * 256:(kk + 1) * 256], scalar1=der[:, w_, 0, kc:kc + 1], scalar2=modv[:, kc, w_:w_ + 1],
                            op0=ALU.mult, op1=ALU.add), reads=[B_c2], writes=[pb, hT_b[kc]])
                    if want_xta:
                        S.op("dve", lambda ps=ps, kk=kk, kc=kc: V.tensor_scalar(
                            out=xTa[:, kc, :], in0=ps[:, kk * 256:(kk + 1) * 256], scalar1=ALPHA, scalar2=None, op0=ALU.mult),
                            writes=[pb, xTa_b[kc]])
                done(pb)

        def proj_fm(wv, wb, jc, ps_ap, pb):
            for kc in range(8):
                S.op("pe", lambda kc=kc: PE.matmul(ps_ap, lhsT=wv[:, kc, jc * 128:(jc + 1) * 128], rhs=hT[:, kc, :],
                                                  start=(kc == 0), stop=(kc == 7)), reads=[wb, hT_b[kc]], writes=[pb])

        def proj_tm(wv, wb, dst, dst_b, eng="act"):
            for j in range(2):
                ps, pb = bank()
                for kc in range(8):
                    S.op("pe", lambda kc=kc, j=j, ps=ps: PE.matmul(ps[:, :], lhsT=hT[:, kc, j * 128:(j + 1) * 128], rhs=wv[:, kc, :],
                                                                start=(kc == 0), stop=(kc == 7)), reads=[wb, hT_b[kc]], writes=[pb])
                db = dst_b[j] if isinstance(dst_b, list) else dst_b
                dbl = db if isinstance(db, list) else [db]
                if eng == "act":
                    S.op("act", lambda j=j, ps=ps: A.activation(out=dst[:, j, :], in_=ps[:, :], func=AF.Copy), writes=[pb] + dbl)
                else:
                    S.op("dve", lambda j=j, ps=ps: V.tensor_copy(out=dst[:, j, :], in_=ps[:, :]), writes=[pb] + dbl)
                done(pb)

        gctr = [0]

        def gates_tanh(ps_ap, pb, h, d_):
            i = d_ * 4 + h
            S.op("act", lambda: A.activation(out=tb[:, i, :], in_=ps_ap, func=AF.Tanh, scale=0.5), writes=[pb, tb_b[i]])
            done(pb)

        gslot = {}

        def gates_ln(h, d_):
            i = d_ * 4 + h
            k0 = (3 * gctr[0]) % 6
            gctr[0] += 1
            s_l = k0
            gslot[(h, d_)] = (k0, k0 + 1, k0 + 2)
            S.op("act", lambda: A.activation(out=gw[:, s_l, :], in_=tb[:, i, :], func=AF.Ln, scale=lbc[:, d_, 0, h:h + 1], bias=lbc[:, d_, 1, h:h + 1]),
                 reads=[tb_b[i], B_c2], writes=[gw_b[s_l]])

        def gates_all(pairs, want_q):
            gates_ln(*pairs[0])
            for n_, (h, d_) in enumerate(pairs):
                if n_ + 1 < len(pairs):
                    gates_ln(*pairs[n_ + 1])
                gates_chain(h, d_, want_q)

        def gates_chain(h, d_, want_q):
            i = d_ * 4 + h
            s_l, s_b, s_r = gslot[(h, d_)]
            if d_ == 0:
                mk = cft[:, CF_NST:CF_NST + TB]
                S.op("dve", lambda: V.tensor_tensor_scan(out=gw[:, s_b, :], data0=mk, data1=gw[:, s_l, :], initial=0.0,
                                                         op0=ALU.mult, op1=ALU.add), reads=[gw_b[s_l], B_const], writes=[gw_b[s_b]])
            else:
                mk = cft[:, CF_NEN:CF_NEN + TB]
                S.op("dve", lambda: V.tensor_tensor_scan(out=gw[:, s_b, ::-1], data0=mk[:, ::-1], data1=gw[:, s_l, ::-1], initial=0.0,
                                                         op0=ALU.mult, op1=ALU.add), reads=[gw_b[s_l], B_const], writes=[gw_b[s_b]])
            S.op("act", lambda: A.activation(out=gw[:, s_l, :], in_=gw[:, s_b, :], func=AF.Exp), reads=[gw_b[s_b]], writes=[gw_b[s_l]])
            S.op("act", lambda: A.activation(out=gw[:, s_r, :], in_=gw[:, s_b, :], func=AF.Exp, scale=-1.0, bias=lnc1[:, d_, h:h + 1]),
                 reads=[gw_b[s_b], B_c2], writes=[gw_b[s_r]])
            ecols = gw[:, s_l, CH - 1::CH] if d_ == 0 else gw[:, s_l, 0::CH]
            S.op("pool", lambda: G.tensor_copy(out=edg[:, d_, h, :], in_=ecols), reads=[gw_b[s_l]], writes=[edg_b[d_][h]])
            S.op("dve", lambda: V.scalar_tensor_tensor(out=kh[:, d_, h, :], in0=tb[:, i, :], scalar=1.0, in1=gw[:, s_r, :],
                                                       op0=ALU.subtract, op1=ALU.mult), reads=[tb_b[i], gw_b[s_r]], writes=[kh_b[d_][h]])
            if want_q:
                S.op("pool", lambda: G.tensor_tensor(out=qh[:, d_, h, :], in0=sq[h][:], in1=gw[:, s_l, :], op=ALU.mult),
                     reads=[sq_b[h], gw_b[s_l]], writes=[qh_b[d_][h]])

        def khT_make(h, d_):
            ps, pb = bank()
            psb = ps[:].bitcast(BF16)
            for j in range(2):
                S.op("pe", lambda j=j: PE.transpose(psb[:, j * 128:(j + 1) * 128], kh[:, d_, h, j * 128:(j + 1) * 128], identb[:]),
                     reads=[kh_b[d_][h], B_c2], writes=[pb])
            S.op("act", lambda: A.activation(out=khT[:, d_, h, :, :], in_=psb[:, 0:256].rearrange("p (j c) -> p j c", c=128), func=AF.Copy),
                 writes=[pb, khT_b[d_][h]])
            done(pb)

        def scores(h, d_):
            ps, pb = bank()
            for m in range(NCHK):
                j, a = m // 2, m % 2
                S.op("pe", lambda m=m, j=j, a=a: PE.matmul(
                    ps[a * 64:(a + 1) * 64, j * 64:(j + 1) * 64], lhsT=kh[:, d_, h, m * 64:(m + 1) * 64], rhs=qh[:, d_, h, m * 64:(m + 1) * 64],
                    start=True, stop=True), reads=[kh_b[d_][h], qh_b[d_][h]], writes=[pb])
            mk = cft[:, CF_MF:CF_MF + 128] if d_ == 0 else cft[:, CF_MB:CF_MB + 128]
            S.op("dve", lambda: V.tensor_tensor(out=scs[:, d_, h, :, :].rearrange("p j c -> p (j c)"), in0=ps[:, 0:128], in1=mk, op=ALU.mult),
                 reads=[B_const], writes=[pb, scs_b[d_][h]])
            done(pb)

        def w_mm(h, d_, m):
            ps, pb = bank()
            j, a = m // 2, m % 2
            S.op("pe", lambda: PE.matmul(ps[:, 0:128], lhsT=khT[a * 64:(a + 1) * 64, d_, h, j, :], rhs=vT[a * 64:(a + 1) * 64, j, h * 128:(h + 1) * 128],
                                         start=True, stop=True), reads=[khT_b[d_][h], vT_b[j]], writes=[pb])
            return ps, pb

        def recur(h, d_, step, m_prev, t0, t0_b, sb_out, sb_out_b, psw, pbw):
            if step == 0:
                tprev, rd, sc = t0, [t0_b], 1.0
            else:
                tprev = Tst[:, d_, h, (step - 1) % 2, :]
                rd = [Tst_b[d_][h][(step - 1) % 2], edg_b[d_][h]]
                sc = edg[:, d_, h, m_prev:m_prev + 1]
            if sb_out is not None:
                S.op("act", lambda: A.activation(out=sb_out, in_=tprev, func=AF.Identity, scale=sc), reads=rd, writes=[sb_out_b])
            S.op("dve", lambda: V.scalar_tensor_tensor(out=Tst[:, d_, h, step % 2, :], in0=tprev, scalar=sc, in1=psw[:, 0:128],
                                                       op0=ALU.mult, op1=ALU.subtract), reads=rd, writes=[pbw, Tst_b[d_][h][step % 2]])
            done(pbw)

        def final_state(h, d_, m_last, dst, dst_b):
            li = (NCHK - 1) % 2
            S.op("act", lambda: A.activation(out=dst, in_=Tst[:, d_, h, li, :], func=AF.Identity, scale=edg[:, d_, h, m_last:m_last + 1]),
                 reads=[Tst_b[d_][h][li], edg_b[d_][h]], writes=[dst_b])

        def pre_block(b, tok0, nxt_tok0):
            pst = load_x(tok0, nxt_tok0)
            evac_x(pst, 0, False)
            wv, wb = w_next("in", 2)
            for h in range(4):
                ps, pb = bank()
                proj_fm(wv, wb, h, ps[:, 0:TB], pb)
                gates_tanh(ps[:, 0:TB], pb, h, 1)
            wv, wb = w_next("in", 3)
            proj_tm(wv, wb, vT, vT_b)
            wv, wb = w_next("in", 5)
            proj_tm(wv, wb, upT, upT_b, eng="dve")
            S.dma("act", ups_sem, ups[tok0:tok0 + TB, :].rearrange("(j p) f -> p j f", p=128), upT, reads=scs_all, writes=[B_ups])
            gates_all([(h, 1) for h in range(4)], False)
            for h in range(4):
                khT_make(h, 1)
            for step, m in enumerate(reversed(range(NCHK))):
                for h in range(4):
                    psw, pbw = w_mm(h, 1, m)
                    recur(h, 1, step, m + 1, carryB[:, h, :], carryB_b[h], None, None, psw, pbw)
            for h in range(4):
                final_state(h, 1, 0, carryB[:, h, :], carryB_b[h])
            if b > 0:
                S.dma("act", bst_sem, bst[b - 1], carryB[:], reads=carryB_b, writes=[B_bst[b - 1]])

        def layer_norm(src_ps_fn, res, res_b, gate_col, outs):
            for j in range(8):
                ps, pb = src_ps_fn(j)
                S.op("dve", lambda ps=ps, j=j: V.scalar_tensor_tensor(out=rr[:, j, :], in0=ps, scalar=gate_col(j), in1=res[:, j, :],
                                                                  op0=ALU.mult, op1=ALU.add), reads=[res_b[j], B_c2], writes=[pb, rr_b[j]])
                done(pb)
                S.op("dve", lambda j=j: V.tensor_copy(out=rbb[:, j, :], in_=rr[:, j, :]), reads=[rr_b[j]], writes=[rbb_b[j]])
                S.op("act", lambda j=j: A.activation(out=rsq[:, j, :], in_=rr[:, j, :], func=AF.Square), reads=[rr_b[j]], writes=[rsq_b[j]])
            pmn, pmnb = bank()
            for j in range(8):
                S.op("pe", lambda j=j: PE.matmul(pmn[:, 0:TB], lhsT=onesb[:, 1, :], rhs=rbb[:, j, :], start=(j == 0), stop=(j == 7)),
                     reads=[rbb_b[j], B_c2], writes=[pmnb])
            pex, pexb = bank()
            for j in range(8):
                S.op("pe", lambda j=j: PE.matmul(pex[:, 0:TB], lhsT=onesb[:, 1, :], rhs=rsq[:, j, :], start=(j == 0), stop=(j == 7)),
                     reads=[rsq_b[j], B_c2], writes=[pexb])
            S.op("act", lambda: A.activation(out=stt[:, 0, :], in_=pmn[:, 0:TB], func=AF.Square), writes=[pmnb, stt_b[0]])
            S.op("dve", lambda: V.tensor_tensor(out=stt[:, 0, :], in0=pex[:, 0:TB], in1=stt[:, 0, :], op=ALU.subtract), writes=[pexb, stt_b[0]])
            done(pexb)
            S.op("act", lambda: A.activation(out=stt[:, 0, :], in_=stt[:, 0, :], func=AF.Ln, bias=epsc[:, 1:2]), reads=[B_c2], writes=[stt_b[0]])
            S.op("act", lambda: A.activation(out=stt[:, 1, :], in_=stt[:, 0, :], func=AF.Exp, scale=-0.5), reads=[stt_b[0]], writes=[stt_b[1]])
            S.op("dve", lambda: V.scalar_tensor_tensor(out=stt[:, 2, :], in0=pmn[:, 0:TB], scalar=-1.0, in1=stt[:, 1, :], op0=ALU.mult, op1=ALU.mult),
                 reads=[stt_b[1]], writes=[pmnb, stt_b[2]])
            done(pmnb)
            for j in range(8):
                S.op("dve", lambda j=j: V.tensor_tensor(out=rr[:, j, :], in0=rr[:, j, :], in1=stt[:, 1, :], op=ALU.mult),
                     reads=[stt_b[1]], writes=[rr_b[j]])
                S.op("pool", lambda j=j: G.tensor_tensor(out=rr[:, j, :], in0=rr[:, j, :], in1=stt[:, 2, :], op=ALU.add),
                     reads=[stt_b[2]], writes=[rr_b[j]])
                for (dst, dstb, scf, bif) in outs:
                    S.op("act", lambda j=j, dst=dst, scf=scf, bif=bif: A.activation(out=dst[:, j, :], in_=rr[:, j, :], func=AF.Identity,
                                                                               scale=scf(j), bias=bif(j)),
                         reads=[rr_b[j], B_c2, B_const], writes=[dstb[j]])

        prefetched = set()

        def fetch_lat(b):
            if ("lat", b) in prefetched:
                return
            prefetched.add(("lat", b))
            if b == NLB - 1:
                S.dma("sp", binit_sem, binit[:], st0[:, 1], writes=[binit_b])
            else:
                S.dma("sp", binit_sem, binit[:], bst[b], reads=[B_bst[b]], writes=[binit_b])
            n_lo, n_hi = max(0, 2 * b - 4), min(32, 2 * b + 6)
            S.dma("sp", uwin_sem, uwin[:, n_lo - (2 * b - 4): n_hi - (2 * b - 4), :],
                  ups[n_lo * 128:n_hi * 128, :].rearrange("(j p) f -> p j f", p=128), reads=[B_ups], writes=[uwin_b])

        def pool_branch(lat, b):
            for g in range(4):
                gi2 = g % 2
                pp, ppb = bank()
                for ml in range(2):
                    if lat:
                        m = 2 * b + ml
                        ins_ = [(_IDX2[(g, dl)], m + dl - (2 * b - 4)) for dl in range(-5, 6) if (g, dl) in _IDX2 and 0 <= m + dl < 32]
                    else:
                        ins_ = [(_IDX1[(g, dl)], ml + dl) for dl in (-1, 0, 1) if (g, dl) in _IDX1 and 0 <= ml + dl < 2]
                    for ii, (pi, ui) in enumerate(ins_):
                        S.op("pe", lambda ml=ml, pi=pi, ui=ui, ii=ii, nn=len(ins_): PE.matmul(
                            pp[:, ml * 128:(ml + 1) * 128], lhsT=ptt[:, pi, :], rhs=uwin[:, ui, g * 128:(g + 1) * 128],
                            start=(ii == 0), stop=(ii == nn - 1)), reads=[uwin_b, B_const], writes=[ppb])
                for ml in range(2):
                    ri = _R2[(g, 2 * b + ml)] if lat else _R1[(g, ml)]
                    ui = (ml + 4) if lat else ml
                    S.op("dve", lambda ml=ml, ri=ri, ui=ui: V.scalar_tensor_tensor(
                        out=dtm[gi2][:, ml, :], in0=pp[:, ml * 128:(ml + 1) * 128], scalar=rct[:, ri:ri + 1], in1=uwin[:, ui, g * 128:(g + 1) * 128],
                        op0=ALU.mult, op1=ALU.subtract), reads=[B_const, uwin_b], writes=[ppb, dtm_b[gi2]])
                done(ppb)
                pt_, ptb = bank()
                ptb16 = pt_[:].bitcast(BF16)
                for ml in range(2):
                    S.op("pe", lambda ml=ml: PE.transpose(ptb16[:, ml * 128:(ml + 1) * 128], dtm[gi2][:, ml, :], identb[:]),
                         reads=[dtm_b[gi2], B_c2], writes=[ptb])
                S.op("act", lambda: A.activation(out=ndf[gi2][:], in_=ptb16[:, 0:TB], func=AF.Copy), writes=[ptb, ndf_b[gi2]])
                done(ptb)
                py, pyb = bank()
                S.op("pe", lambda: PE.matmul(py[:, 0:TB], lhsT=poolwb[:, g, :], rhs=ndf[gi2][:], start=True, stop=True),
                     reads=[ndf_b[gi2], B_c2], writes=[pyb])
                S.op("act", lambda: A.activation(out=ymix[:, 4 + g, :], in_=py[:, 0:TB], func=AF.Identity, scale=nps[:, g:g + 1]),
                     reads=[B_c2], writes=[pyb, ymix_b[4 + g]])
                done(pyb)

        def A_gen(kind, b, tok0, nxt_tok0, nxt):
            lat = kind == "lat"
            w_ = 0 if lat else 1
            if lat:
                fetch_lat(b)
                if b == 0:
                    S.dma("sp", misc_sem, carryF[:], st0[:, 0], writes=carryF_b)
            pst = load_x(tok0, nxt_tok0)
            evac_x(pst, w_, True)
            yield
            wv, wb = w_next("in", 0)
            for h in range(4):
                ps, pb = bank()
                proj_fm(wv, wb, h, ps[:, 0:TB], pb)
                S.op("act", lambda ps=ps, h=h: A.activation(out=sq[h][:], in_=ps[:, 0:TB], func=AF.Silu), writes=[pb, sq_b[h]])
                done(pb)
            yield
            for d_ in range(2):
                wv, wb = w_next("in", 1 + d_)
                for h in range(4):
                    ps, pb = bank()
                    proj_fm(wv, wb, h, ps[:, 0:TB], pb)
                    gates_tanh(ps[:, 0:TB], pb, h, d_)
                yield
            wv, wb = w_next("in", 3)
            proj_tm(wv, wb, vT, vT_b)
            yield
            wv, wb = w_next("in", 4)
            for h in range(4):
                ps, pb = bank()
                proj_fm(wv, wb, h, ps[:, 0:TB], pb)
                S.op("act", lambda ps=ps, h=h: A.activation(out=sgate[:, h, :], in_=ps[:, 0:TB], func=AF.Silu), writes=[pb, sgate_b[h]])
                done(pb)
            if not lat:
                wv, wb = w_next("in", 5)
                proj_tm(wv, wb, uwin, uwin_b)
            yield
            pool_branch(lat, b)
            yield
            gates_all([(h, d_) for d_ in range(2) for h in range(4)], True)
            yield
            for d_ in range(2):
                for h in range(4):
                    khT_make(h, d_)
                    scores(h, d_)
                yield
            ni = b % 2
            for step, m in enumerate(reversed(range(NCHK))):
                for h in range(4):
                    t0, t0_b = (binit[:, h, :], binit_b) if lat else (zst[:], B_c2)
                    psw, pbw = w_mm(h, 1, m)
                    recur(h, 1, step, m + 1, t0, t0_b, Sbb[:, h, m, :], Sbb_b[h][m], psw, pbw)
                if step < NCHK - 1:
                    yield
            if not lat:
                for h in range(4):
                    final_state(h, 1, 0, nsst[ni][:, h, :], nsst_hb(ni, h))
            yield
            pos = [bank() for h in range(4)]
            for m in range(NCHK):
                j, a = m // 2, m % 2
                for h in range(4):
                    po, pob = pos[h]
                    t0, t0_b = (carryF[:, h, :], carryF_b[h]) if lat else (zst[:], B_c2)
                    psw, pbw = w_mm(h, 0, m)
                    recur(h, 0, m, m - 1, t0, t0_b, Sbf[:, h, m % 2, :], Sbf_b[h][m % 2], psw, pbw)
                    oc = po[:, m * 64:(m + 1) * 64]
                    S.op("pe", lambda m=m, oc=oc, h=h: PE.matmul(oc, lhsT=Sbf[:, h, m % 2, :], rhs=qh[:, 0, h, m * 64:(m + 1) * 64], start=True, stop=False),
                         reads=[Sbf_b[h][m % 2], qh_b[0][h]], writes=[pob])
                    S.op("pe", lambda m=m, oc=oc, h=h: PE.matmul(oc, lhsT=vT[a * 64:(a + 1) * 64, j, h * 128:(h + 1) * 128],
                                                               rhs=scs[a * 64:(a + 1) * 64, 0, h, j, :], start=False, stop=False),
                         reads=[vT_b[j], scs_b[0][h]], writes=[pob])
                    S.op("pe", lambda m=m, oc=oc, h=h: PE.matmul(oc, lhsT=Sbb[:, h, m, :], rhs=qh[:, 1, h, m * 64:(m + 1) * 64], start=False, stop=False),
                         reads=[Sbb_b[h][m], qh_b[1][h]], writes=[pob])
                    S.op("pe", lambda m=m, oc=oc, h=h: PE.matmul(oc, lhsT=vT[a * 64:(a + 1) * 64, j, h * 128:(h + 1) * 128],
                                                               rhs=scs[a * 64:(a + 1) * 64, 1, h, j, :], start=False, stop=True),
                         reads=[vT_b[j], scs_b[1][h]], writes=[pob])
                yield
            for h in range(4):
                if h == 2:
                    yield
                po, pob = pos[h]
                final_state(h, 0, NCHK - 1, carryF[:, h, :], carryF_b[h])
                S.op("act", lambda h=h, po=po: A.activation(out=osb[h][:], in_=po[:, 0:TB], func=AF.Copy), writes=[pob, osb_b[h]])
                S.op("act", lambda h=h, po=po: A.activation(out=osq[h % 2][:], in_=po[:, 0:TB], func=AF.Square), writes=[pob, osq_b[h % 2]])
                done(pob)
                pm, pmb = bank()
                S.op("pe", lambda h=h, pm=pm: PE.matmul(pm[:, 0:TB], lhsT=onesb[:, 0, :], rhs=osq[h % 2][:], start=True, stop=True),
                     reads=[osq_b[h % 2], B_c2], writes=[pmb])
                S.op("act", lambda h=h, pm=pm: A.activation(out=rstd_t[:, h, :], in_=pm[:, 0:TB], func=AF.Ln, bias=epsc[:, 0:1]),
                     reads=[B_c2], writes=[pmb, rstd_b])
                done(pmb)
                S.op("dve", lambda h=h: V.scalar_tensor_tensor(out=osb[h][:], in0=osb[h][:], scalar=prmt[:, PRM_GN:PRM_GN + 1], in1=sgate[:, h, :],
                                                           op0=ALU.mult, op1=ALU.mult), reads=[B_const, sgate_b[h]], writes=[osb_b[h]])
            if nxt is not None and nxt[0] == "lat":
                fetch_lat(nxt[1])
            yield
            S.op("act", lambda: A.activation(out=rstd_t[:], in_=rstd_t[:], func=AF.Exp, scale=-0.5), writes=[rstd_b])
            for h in range(4):
                S.op("dve", lambda h=h: V.tensor_tensor(out=ymix[:, h, :], in0=osb[h][:], in1=rstd_t[:, h, :], op=ALU.mult),
                     reads=[osb_b[h], rstd_b], writes=[ymix_b[h]])
            if not lat:
                S.dma("act", nsf_sem, nsf[b].rearrange("h d e -> d h e"), carryF[:], reads=carryF_b)
                S.dma("act", nsb_sems[ni], nsb[b].rearrange("h d e -> d h e"), nsst[ni][:], reads=(carryB_b if ni == 0 else [binit_b]))

        def F_wout_ln1(w_):
            wo = {}

            def wout_ps(j):
                if j // 4 not in wo:
                    wo[j // 4] = w_next("out", j // 4)
                ps, pb = bank()
                wv, wb = wo[j // 4]
                jc = j % 4
                for kc in range(8):
                    S.op("pe", lambda kc=kc: PE.matmul(ps[:, 0:TB], lhsT=wv[:, kc, jc * 128:(jc + 1) * 128], rhs=ymix[:, kc, :],
                                                      start=(kc == 0), stop=(kc == 7)), reads=[wb, ymix_b[kc]], writes=[pb])
                return ps[:, 0:TB], pb

            layer_norm(wout_ps, xTa, xTa_b, lambda j: modv[:, 16 + j, w_:w_ + 1],
                       [(h2, h2_b, lambda j: der[:, w_, 1, j:j + 1], lambda j: der[:, w_, 2, j:j + 1]),
                        (x2a, x2a_b, lambda j: shr[:, 0, j:j + 1], lambda j: shr[:, 1, j:j + 1])])

        gu_open = {}

        def F_gu_jj(u, jl):
            if jl == 0:
                gu_open["g"] = w_next("g", u)
                gu_open["u"] = w_next("u", u)
            wgv, wgb_ = gu_open["g"]
            wuv, wub_ = gu_open["u"]
            jj = u * 4 + jl
            ps, pb = bank()
            for kc in range(8):
                S.op("pe", lambda kc=kc: PE.matmul(ps[:, 0:TB], lhsT=wgv[:, kc, jl * 128:(jl + 1) * 128], rhs=h2[:, kc, :],
                                                  start=(kc == 0), stop=(kc == 7)), reads=[wgb_, h2_b[kc]], writes=[pb])
            ps2, pb2 = bank()
            for kc in range(8):
                S.op("pe", lambda kc=kc: PE.matmul(ps2[:, 0:TB], lhsT=wuv[:, kc, jl * 128:(jl + 1) * 128], rhs=h2[:, kc, :],
                                                  start=(kc == 0), stop=(kc == 7)), reads=[wub_, h2_b[kc]], writes=[pb2])
            gi3 = jj % 2
            S.op("act", lambda: A.activation(out=sgt[gi3][:], in_=ps[:, 0:TB], func=AF.Silu), writes=[pb, sgt_b[gi3]])
            done(pb)
            S.op("dve", lambda: V.tensor_tensor(out=act[:, jj, :], in0=ps2[:, 0:TB], in1=sgt[gi3][:], op=ALU.mult),
                 reads=[sgt_b[gi3]], writes=[pb2, act_b[jj]] + stg_b)
            done(pb2)

        def F_dn_ln2(w_, hook):
            dn_cache = {}

            def down_ps(j):
                if j > 0:
                    hook()
                jp, jl = j // 2, j % 2
                if jp not in dn_cache:
                    dn_cache.clear()
                    dn_cache[jp] = [w_next("d", jp, 0), w_next("d", jp, 1)]
                ps, pb = bank()
                for hf in range(2):
                    wv, wb = dn_cache[jp][hf]
                    for kl in range(11):
                        kk = hf * 11 + kl
                        S.op("pe", lambda kl=kl, kk=kk, wv=wv: PE.matmul(ps[:, 0:TB], lhsT=wv[:, kl, jl * 128:(jl + 1) * 128], rhs=act[:, kk, :],
                                                                       start=(kk == 0), stop=(kk == 21)), reads=[wb, act_b[kk]], writes=[pb])
                return ps[:, 0:TB], pb

            layer_norm(down_ps, x2a, x2a_b, lambda j: modv[:, 40 + j, w_:w_ + 1],
                       [(outF, outF_b, lambda j: prmt[:, PRM_L2G + j:PRM_L2G + j + 1], lambda j: prmt[:, PRM_L2B + j:PRM_L2B + j + 1])])

        def F_out(tok0):
            for j in range(2):
                for half in range(2):
                    ps, pb = bank()
                    for kl in range(4):
                        kc = half * 4 + kl
                        S.op("pe", lambda kl=kl, kc=kc: PE.transpose(ps[:, kl * 128:(kl + 1) * 128], outF[:, kc, j * 128:(j + 1) * 128],
                                                                    cft[:, CF_ID:CF_ID + 128]), reads=[outF_b[kc], B_const], writes=[pb])
                    S.op("act", lambda: A.activation(out=stg[j][:, half * 512:(half + 1) * 512], in_=ps[:, :], func=AF.Copy),
                         writes=[pb, stg_b[j]] + act_b)
                    done(pb)
                S.dma("act", stg_sem[j], y[tok0 + j * 128: tok0 + (j + 1) * 128, :], stg[j][:], reads=[stg_b[j]])

        S.dma("sp", misc_sem, carryB[:], st0[:, 1], writes=carryB_b)
        w_issue()

        def tok_of(kind, b):
            return b * TB if kind in ("pre", "lat") else LAT_T + b * PR_T

        issue_x(tok_of(*plan[0]))
        pre = [p for p in plan if p[0] == "pre"]
        mains = [p for p in plan if p[0] != "pre"]
        for pi_, (kind, b) in enumerate(pre):
            nxt = plan[pi_ + 1]
            pre_block(b, tok_of(kind, b), tok_of(*nxt))

        def mk_A(i):
            kind, b = mains[i]
            nxt = mains[i + 1] if i + 1 < len(mains) else None
            return A_gen(kind, b, tok_of(kind, b), tok_of(*nxt) if nxt is not None else None, nxt)

        def drain(gen, n=None):
            k = 0
            while gen is not None and (n is None or k < n):
                try:
                    next(gen)
                except StopIteration:
                    return None
                k += 1
            return gen

        drain(mk_A(0))
        for i, (kind, b) in enumerate(mains):
            w_ = 0 if kind == "lat" else 1
            F_wout_ln1(w_)
            if i > 0:
                F_out(tok_of(*mains[i - 1]))
            hold = [mk_A(i + 1) if i + 1 < len(mains) else None]

            def adv(n=1):
                hold[0] = drain(hold[0], n)

            adv(2)
            for u in range(6):
                for jl in range(4 if u < 5 else 2):
                    F_gu_jj(u, jl)
                    if u >= 2:
                        adv()
                if u < 2:
                    adv(2)
            F_dn_ln2(w_, adv)
            drain(hold[0])
        F_out(tok_of(*mains[-1]))
        for s_ in stg_sem + [nsf_sem] + nsb_sems:
            SP.wait_ge(S.sems[s_], S.cnt[s_])
        build_program.stats = (dict(S.cnt), S.nwait)
        build_program.order = list(wq)
    return nc


_NC_CACHE = {}


def _prep_inputs(inp):
    f = lambda a: np.ascontiguousarray(np.asarray(a, dtype=np.float32))
    xp, xsm = f(inp["x_prompt"]), f(inp["x_sample"])
    sf, sbk = f(inp["state_hgrn_fwd"]), f(inp["state_hgrn_bwd"])
    c, cctx = f(inp["c"]), f(inp["c_ctx"])
    shared = {
        "w_mod": f(inp["w_mod"][0]),
        "b_mod": np.ascontiguousarray(f(inp["b_mod"][0]).reshape(48, 128).T),
        "w_in": f(inp["w_in"][0]),
        "w_out": f(inp["w_out"][0]),
        "wg": f(inp["w_ffn_gate"][0]),
        "wu": f(inp["w_ffn_up"][0]),
        "wd": f(inp["w_ffn_down"][0]),
        "poolw": np.ascontiguousarray(f(inp["pool_w"][0]).transpose(1, 0, 2)),
        "cf": _CF,
        "ptab": _PT,
        "rtab": _RC,
    }
    prm = np.zeros((128, PRM_N), np.float32)
    lf, lb_ = f(inp["lb_fwd_raw"]), f(inp["lb_bwd_raw"])
    prm[:, PRM_LBF:PRM_LBF + 8] = lf.reshape(2, 4, 128).transpose(2, 1, 0).reshape(128, 8)
    prm[:, PRM_LBB:PRM_LBB + 8] = lb_.reshape(2, 4, 128).transpose(2, 1, 0).reshape(128, 8)
    prm[:, PRM_GN] = f(inp["hgrn_norm_g"][0])
    prm[:, PRM_PS:PRM_PS + 4] = f(inp["pool_scale"][0]).reshape(4, 128).T
    prm[:, PRM_L1G:PRM_L1G + 8] = f(inp["ln1_g"][0]).reshape(8, 128).T
    prm[:, PRM_L1B:PRM_L1B + 8] = f(inp["ln1_b"][0]).reshape(8, 128).T
    prm[:, PRM_L2G:PRM_L2G + 8] = f(inp["ln2_g"][0]).reshape(8, 128).T
    prm[:, PRM_L2B:PRM_L2B + 8] = f(inp["ln2_b"][0]).reshape(8, 128).T
    shared["prm"] = prm
    maps = []
    for i in range(NCORES):
        m = dict(shared)
        m["xs"] = np.concatenate([xsm[i], xp[i * NPR:(i + 1) * NPR].reshape(NPR * PR_T, D)], 0)
        st = np.stack([sf[i, 0], sbk[i, 0]], 0)
        m["st0"] = np.ascontiguousarray(st.transpose(2, 0, 1, 3))
        cv = np.stack([c[i], cctx], -1)
        m["cvec"] = np.ascontiguousarray(cv.reshape(8, 128, 2).transpose(1, 0, 2))
        maps.append(m)
    return maps


def kernel(**inputs):
    if "nc" not in _NC_CACHE:
        build_program()
        _NC_CACHE["nc"] = build_program(build_program.order)
    nc = _NC_CACHE["nc"]
    maps = _prep_inputs(inputs)
    res = run_bass_kernel_spmd(nc, maps, core_ids=list(range(NCORES)))
    ys = [np.asarray(r["y"], dtype=np.float32) for r in res.results]
    y_sample = np.stack([a[:LAT_T] for a in ys], 0)
    y_prompt = np.concatenate([a[LAT_T:].reshape(NPR, PR_T, D) for a in ys], 0)
    nsf = np.concatenate([np.asarray(r["nsf"], dtype=np.float32) for r in res.results], 0)[:, None]
    nsb = np.concatenate([np.asarray(r["nsb"], dtype=np.float32) for r in res.results], 0)[:, None]
    return (y_prompt, y_sample, nsf, nsb)
```

```python
import numpy as np
import ml_dtypes
from contextlib import ExitStack
import concourse.bass as bass
import concourse.mybir as mybir
from concourse.bass_utils import run_bass_kernel_spmd

F32 = mybir.dt.float32
BF16 = mybir.dt.bfloat16
AF = mybir.ActivationFunctionType
ALU = mybir.AluOpType

NCORES = 8
D = 1024
DFF = 2816
TB = 256
CH = 64
NCHK = TB // CH
LAT_T = 4096
PR_T = 256
NPR = 4
TOK = LAT_T + NPR * PR_T
ALPHA = 2.0 ** 0.25
LN_EPS = 1e-5
RMS_EPS = 1e-6
WINS = (2, 4, 8, 16)
NSLOT = 4


def _pool_tables():
    P = []
    idx2 = {}
    idx1 = {}
    for g, w in enumerate(WINS):
        h = w // 2
        for dl in range(-5, 6):
            m = np.zeros((128, 128), np.float32)
            for a in range(2):
                for a2 in range(2):
                    if a2 - h <= 2 * dl + a < a2 + h:
                        for c2 in range(64):
                            lo, hi = max(c2 - h, 0), min(c2 + h, 64)
                            m[a * 64 + lo:a * 64 + hi, a2 * 64 + c2] = 1.0
            if m.any():
                idx2[(g, dl)] = len(P)
                P.append(m)
        for dl in (-1, 0, 1):
            m = np.zeros((128, 128), np.float32)
            for i2 in range(128):
                lo, hi = i2 - h - 128 * dl, i2 + h - 128 * dl
                lo, hi = max(lo, 0), min(hi, 128)
                if hi > lo:
                    m[lo:hi, i2] = 1.0
            if m.any():
                idx1[(g, dl)] = len(P)
                P.append(m)
    R = []
    rkey = {}
    r2 = {}
    r1 = {}

    def add(v):
        k = v.tobytes()
        if k not in rkey:
            rkey[k] = len(R)
            R.append(v)
        return rkey[k]

    for g, w in enumerate(WINS):
        h = w // 2
        cc = np.array([min(c + h, 64) - max(c - h, 0) for c in range(64)], np.float32)
        for m in range(32):
            v = np.zeros(128, np.float32)
            for a in range(2):
                r = 2 * m + a
                rc = min(r + h, 64) - max(r - h, 0)
                v[a * 64:(a + 1) * 64] = 1.0 / (rc * cc)
            r2[(g, m)] = add(v)
        for m in range(2):
            t = np.arange(128) + 128 * m
            cnt = np.minimum(t + h, 256) - np.maximum(t - h, 0)
            r1[(g, m)] = add((1.0 / cnt).astype(np.float32))
    PT = np.stack(P, 1)
    RC = np.ascontiguousarray(np.stack(R, 1))
    return PT.astype(ml_dtypes.bfloat16), RC.astype(np.float32), idx2, idx1, r2, r1


_PT, _RC, _IDX2, _IDX1, _R2, _R1 = _pool_tables()
NPT = _PT.shape[1]
NRC = _RC.shape[1]


def _const_f32():
    ident = np.eye(128, dtype=np.float32)
    p = np.arange(128) % 64
    c = np.arange(64)
    mF = np.where(c[None, :] >= p[:, None], -1.0, 0.0).astype(np.float32)
    mB = np.where(c[None, :] <= p[:, None], -1.0, 0.0).astype(np.float32)
    mF = np.repeat(mF[:, None, :], 2, 1).reshape(128, 128)
    mB = np.repeat(mB[:, None, :], 2, 1).reshape(128, 128)
    st = np.zeros((128, TB), np.float32)
    st[:, ::CH] = 1.0
    en = np.zeros((128, TB), np.float32)
    en[:, CH - 1::CH] = 1.0
    return np.concatenate([ident, mF, mB, st, en, 1.0 - st, 1.0 - en], 1)


_CF = _const_f32()
CF_ID, CF_MF, CF_MB, CF_ST, CF_EN, CF_NST, CF_NEN = 0, 128, 256, 384, 384 + TB, 384 + 2 * TB, 384 + 3 * TB
PRM_LBF, PRM_LBB, PRM_GN, PRM_PS, PRM_L1G, PRM_L1B, PRM_L2G, PRM_L2B, PRM_N = 0, 8, 16, 17, 21, 29, 37, 45, 53


class Buf:
    __slots__ = ("w", "r", "name")

    def __init__(self, name=""):
        self.w = None
        self.r = {}
        self.name = name


class Sched:
    def __init__(self, nc, es):
        self.nc = nc
        self.eng = {"pe": nc.tensor, "act": nc.scalar, "dve": nc.vector, "pool": nc.gpsimd, "sp": nc.sync}
        self.sems = {}
        self.cnt = {}
        self.waited = {e: {} for e in self.eng}
        self.es = es
        for e in self.eng:
            self.sems[e] = es.enter_context(nc.semaphore("prog_" + e))
            self.cnt[e] = 0
        self.nwait = 0

    def new_sem(self, name):
        s = self.es.enter_context(self.nc.semaphore(name))
        self.sems[name] = s
        self.cnt[name] = 0
        return name

    def _deps(self, e, reads, writes):
        deps = {}

        def add(tok):
            if tok is None:
                return
            k, v = tok
            if k == e and e == "pe":
                return
            if deps.get(k, 0) < v:
                deps[k] = v

        for b in reads:
            add(b.w)
        for b in writes:
            add(b.w)
            for k, v in b.r.items():
                add((k, v))
        out = []
        for k, v in deps.items():
            if self.waited[e].get(k, 0) >= v:
                continue
            self.waited[e][k] = v
            out.append((k, v))
        return out

    def _emit(self, e, fn, deps):
        eng = self.eng[e]
        for k, v in deps[1:]:
            eng.wait_ge(self.sems[k], v)
            self.nwait += 1
        ins = fn()
        if deps:
            ins._wait_ge(self.sems[deps[0][0]], deps[0][1])
        return ins

    def op(self, e, fn, reads=(), writes=()):
        deps = self._deps(e, reads, writes)
        ins = self._emit(e, fn, deps)
        self.cnt[e] += 1
        ins.then_inc(self.sems[e], 1)
        tok = (e, self.cnt[e])
        for b in reads:
            if b.r.get(e, 0) < tok[1]:
                b.r[e] = tok[1]
        for b in writes:
            b.w = tok
            b.r = {}
        return ins

    def dma(self, q, sem, out, in_, reads=(), writes=(), **kw):
        deps = self._deps(q, reads, writes)
        eng = self.eng[q]
        ins = self._emit(q, lambda: eng.dma_start(out=out, in_=in_, **kw), deps)
        self.cnt[sem] += 16
        ins.then_inc(self.sems[sem], 16)
        tok = (sem, self.cnt[sem])
        for b in reads:
            if b.r.get(sem, 0) < tok[1]:
                b.r[sem] = tok[1]
        for b in writes:
            b.w = tok
            b.r = {}
        return ins


def build_program(order=None):
    nc = bass.Bass("TRN2", target_bir_lowering=False)

    def din(name, shape, dt=F32):
        return nc.dram_tensor(name, list(shape), dt, kind="ExternalInput").ap()

    xs = din("xs", [TOK, D])
    st0 = din("st0", [128, 2, 4, 128])
    cvec = din("cvec", [128, 8, 2])
    w_mod = din("w_mod", [D, 6 * D])
    b_mod = din("b_mod", [128, 48])
    w_in = din("w_in", [D, 3 * D])
    w_out = din("w_out", [D, D])
    wg = din("wg", [D, DFF])
    wu = din("wu", [D, DFF])
    wd = din("wd", [DFF, D])
    poolw = din("poolw", [128, 4, 128])
    prm = din("prm", [128, PRM_N])
    cf = din("cf", list(_CF.shape))
    ptab = din("ptab", [128, NPT, 128], BF16)
    rtab = din("rtab", [128, NRC])
    y = nc.dram_tensor("y", [TOK, D], F32, kind="ExternalOutput").ap()
    nsf = nc.dram_tensor("nsf", [NPR, 4, 128, 128], F32, kind="ExternalOutput").ap()
    nsb = nc.dram_tensor("nsb", [NPR, 4, 128, 128], F32, kind="ExternalOutput").ap()
    w_in_b = nc.dram_tensor("w_in_b", [D, 3 * D], BF16).ap()
    w_out_b = nc.dram_tensor("w_out_b", [D, D], BF16).ap()
    wg_b = nc.dram_tensor("wg_b", [D, DFF], BF16).ap()
    wu_b = nc.dram_tensor("wu_b", [D, DFF], BF16).ap()
    wd_b = nc.dram_tensor("wd_b", [DFF, D], BF16).ap()
    ups = nc.dram_tensor("ups", [LAT_T, 512], BF16).ap()
    bst = nc.dram_tensor("bst", [LAT_T // TB, 128, 4, 128], F32).ap()

    es = ExitStack()
    with es:
        S = Sched(nc, es)

        def sb(name, shape, dt=F32):
            return es.enter_context(nc.sbuf_tensor(name, list(shape), dt))

        def bufs(n, name):
            return [Buf(f"{name}{i}") for i in range(n)]

        cft = sb("cft", _CF.shape)
        identb = sb("identb", [128, 128], BF16)
        onesb = sb("onesb", [128, 2, 128], BF16)
        ptt = sb("ptt", [128, NPT, 128], BF16)
        rct = sb("rct", [128, NRC])
        lnc1 = sb("lnc1", [128, 2, 4])
        epsc = sb("epsc", [128, 2])
        prmt = sb("prmt", [128, PRM_N])
        poolwf = sb("poolwf", [128, 4, 128])
        poolwb = sb("poolwb", [128, 4, 128], BF16)
        cvt = sb("cvt", [128, 16])
        scv = sb("scv", [128, 16])
        bmt = sb("bmt", [128, 48])
        modv = sb("modv", [128, 48, 2])
        der = sb("der", [128, 2, 3, 8])
        shr = sb("shr", [128, 2, 8])
        lbc = sb("lbc", [128, 2, 2, 4])
        lbt = sb("lbt", [128, 2, 4])
        nps = sb("nps", [128, 4])
        B_const = Buf("const")

        slots = [sb(f"wslot{i}", [128, 2048]) for i in range(NSLOT)]
        slot_b = bufs(NSLOT, "wslot")
        slot_sem = [S.new_sem(f"wsem{i}") for i in range(NSLOT)]

        xin = [sb(f"xin{i}", [128, D]) for i in range(2)]
        xin_b = bufs(2, "xin")
        xin_sem = [S.new_sem(f"xsem{i}") for i in range(2)]
        xTa = sb("xTa", [128, 8, TB])
        xTa_b = bufs(8, "xTa")
        hT = sb("hT", [128, 8, TB], BF16)
        hT_b = bufs(8, "hT")
        tb = sb("tb", [128, 8, TB])
        tb_b = bufs(8, "tb")
        gw = sb("gw", [128, 6, TB])
        gw_b = bufs(6, "gw")
        sq = [sb(f"sq{i}", [128, TB]) for i in range(4)]
        sq_b = bufs(4, "sq")
        edg = sb("edg", [128, 2, 4, NCHK])
        edg_b = [bufs(4, f"edg{d}_") for d in range(2)]
        qh = sb("qh", [128, 2, 4, TB], BF16)
        qh_b = [bufs(4, f"qh{d}_") for d in range(2)]
        kh = sb("kh", [128, 2, 4, TB], BF16)
        kh_b = [bufs(4, f"kh{d}_") for d in range(2)]
        khT = sb("khT", [128, 2, 4, 2, 128], BF16)
        khT_b = [bufs(4, f"khT{d}_") for d in range(2)]
        vT = sb("vT", [128, 2, 512], BF16)
        vT_b = bufs(2, "vT")
        sgate = sb("sgate", [128, 4, TB], BF16)
        sgate_b = bufs(4, "sgate")
        scs = sb("scs", [128, 2, 4, 2, CH], BF16)
        scs_b = [bufs(4, f"scs{d}_") for d in range(2)]
        Tst = sb("Tst", [128, 2, 4, 2, 128])
        Tst_b = [[bufs(2, f"T{d}{h}_") for h in range(4)] for d in range(2)]
        Sbb = sb("Sbb", [128, 4, NCHK, 128], BF16)
        Sbb_b = [bufs(NCHK, f"Sbb{h}_") for h in range(4)]
        Sbf = sb("Sbf", [128, 4, 2, 128], BF16)
        Sbf_b = [bufs(2, f"Sbf{h}_") for h in range(4)]
        carryF = sb("carryF", [128, 4, 128])
        carryF_b = bufs(4, "carryF")
        carryB = sb("carryB", [128, 4, 128])
        carryB_b = bufs(4, "carryB")
        binit = sb("binit", [128, 4, 128])
        binit_b = Buf("binit")
        binit_sem = S.new_sem("binit_sem")
        zst = sb("zst", [128, 128])
        osb = [sb(f"osb{i}", [128, TB]) for i in range(4)]
        osb_b = bufs(4, "osb")
        osq = [sb(f"osq{i}", [128, TB], BF16) for i in range(2)]
        osq_b = bufs(2, "osq")
        rstd_t = sb("rstd_t", [128, 4, TB])
        rstd_b = Buf("rstd")
        ymix = sb("ymix", [128, 8, TB], BF16)
        ymix_b = bufs(8, "ymix")
        uwin = sb("uwin", [128, 10, 512], BF16)
        uwin_b = Buf("uwin")
        uwin_sem = S.new_sem("uwin_sem")
        dtm = [sb(f"dtm{i}", [128, 2, 128], BF16) for i in range(2)]
        dtm_b = bufs(2, "dtm")
        ndf = [sb(f"ndf{i}", [128, TB], BF16) for i in range(2)]
        ndf_b = bufs(2, "ndf")
        rr = sb("rr", [128, 8, TB])
        rr_b = bufs(8, "rr")
        rbb = sb("rbb", [128, 8, TB], BF16)
        rbb_b = bufs(8, "rbb")
        rsq = sb("rsq", [128, 8, TB], BF16)
        rsq_b = bufs(8, "rsq")
        stt = sb("stt", [128, 4, TB])
        stt_b = bufs(4, "stt")
        x2a = sb("x2a", [128, 8, TB])
        x2a_b = bufs(8, "x2a")
        h2 = sb("h2", [128, 8, TB], BF16)
        h2_b = bufs(8, "h2")
        act = sb("act", [128, 22, TB], BF16)
        act_b = bufs(22, "act")
        sgt = [sb(f"sgt{i}", [128, TB]) for i in range(2)]
        sgt_b = bufs(2, "sgt")
        outF = sb("outF", [128, 8, TB])
        outF_b = bufs(8, "outF")
        actf = act[:].rearrange("p a b -> p (a b)").bitcast(F32)
        stg = [actf[:, i * D:(i + 1) * D] for i in range(2)]
        stg_b = bufs(2, "stg")
        stg_sem = [S.new_sem(f"stgsem{i}") for i in range(2)]
        misc_sem = S.new_sem("misc_sem")
        ups_sem = S.new_sem("ups_sem")
        bst_sem = S.new_sem("bst_sem")
        nsf_sem = S.new_sem("nsf_sem")
        nsb_sems = [S.new_sem(f"nsb_sem{i}") for i in range(2)]
        cast_sems = {k: S.new_sem("cast_" + k) for k in ("in", "out", "g", "u", "d")}
        B_ups = Buf("ups_dram")
        B_bst = [Buf(f"bst{i}") for i in range(LAT_T // TB)]
        B_wb = {k: Buf("wb_" + k) for k in ("in", "out", "g", "u", "d")}
        upT = scs[:].rearrange("p a b c d -> p (a b c d)").rearrange("p (j f) -> p j f", f=512)
        scs_all = scs_b[0] + scs_b[1]
        upT_b = [scs_all, scs_all]
        nsst = [carryB, binit]

        def nsst_hb(ni, h):
            return carryB_b[h] if ni == 0 else binit_b

        psum = [es.enter_context(nc.psum_tensor(f"ps{i}", [128, 512], F32)) for i in range(8)]
        ps_b = bufs(8, "ps")
        pctr = [0]

        live = set()

        def bank():
            for _ in range(8):
                i = pctr[0] % 8
                pctr[0] += 1
                if i not in live:
                    live.add(i)
                    return psum[i], ps_b[i]
            raise RuntimeError("no free PSUM bank")

        def done(pb):
            live.discard(ps_b.index(pb))

        block = es.enter_context(nc.Block())
        V, A, G, PE, SP = nc.vector, nc.scalar, nc.gpsimd, nc.tensor, nc.sync

        def cast(dst, src, rows, cols, key):
            for c0 in range(0, cols, 1024):
                c1 = min(cols, c0 + 1024)
                for r0 in range(0, rows, 512):
                    r1 = min(rows, r0 + 512)
                    S.dma("pool", cast_sems[key], dst[r0:r1, c0:c1], src[r0:r1, c0:c1], writes=[B_wb[key]])

        cast(w_in_b, w_in, D, 3 * D, "in")

        S.dma("sp", misc_sem, cft[:], cf[:], writes=[B_const])
        S.dma("sp", misc_sem, ptt[:], ptab[:], writes=[B_const])
        S.dma("sp", misc_sem, rct[:], rtab[:], writes=[B_const])
        S.dma("sp", misc_sem, prmt[:], prm[:], writes=[B_const])
        S.dma("sp", misc_sem, poolwf[:], poolw[:], writes=[B_const])
        S.dma("sp", misc_sem, cvt[:], cvec.rearrange("p a b -> p (a b)"), writes=[B_const])
        S.dma("sp", misc_sem, bmt[:], b_mod[:], writes=[B_const])
        B_c2 = Buf("const2")
        S.op("dve", lambda: V.tensor_copy(out=identb[:], in_=cft[:, CF_ID:CF_ID + 128]), reads=[B_const], writes=[B_c2])
        S.op("dve", lambda: V.memset(onesb[:, 0, :], 1.0 / 128.0), writes=[B_c2])
        S.op("dve", lambda: V.memset(onesb[:, 1, :], 1.0 / 1024.0), writes=[B_c2])
        S.op("dve", lambda: V.memset(zst[:], 0.0), writes=[B_c2])
        S.op("dve", lambda: V.tensor_copy(out=poolwb[:], in_=poolwf[:]), reads=[B_const], writes=[B_c2])
        S.op("act", lambda: A.activation(out=scv[:], in_=cvt[:], func=AF.Silu), reads=[B_const], writes=[B_c2])
        for d_, col in ((0, PRM_LBF), (1, PRM_LBB)):
            rv = prmt[:, col:col + 8].rearrange("p (h r) -> p h r", r=2)
            S.op("dve", lambda rv=rv, d_=d_: V.tensor_tensor(out=lbt[:, d_, :], in0=rv[:, :, 0], in1=rv[:, :, 1], op=ALU.subtract),
                 reads=[B_const], writes=[B_c2])
            S.op("act", lambda d_=d_: A.activation(out=lbt[:, d_, :], in_=lbt[:, d_, :], func=AF.Tanh, scale=0.5),
                 reads=[B_c2], writes=[B_c2])
            S.op("dve", lambda d_=d_: V.tensor_scalar(out=lbc[:, d_, 0, :], in0=lbt[:, d_, :], scalar1=-0.25, scalar2=0.25,
                                                    op0=ALU.mult, op1=ALU.add), reads=[B_c2], writes=[B_c2])
            S.op("dve", lambda d_=d_: V.tensor_scalar(out=lbc[:, d_, 1, :], in0=lbt[:, d_, :], scalar1=0.25, scalar2=0.75,
                                                    op0=ALU.mult, op1=ALU.add), reads=[B_c2], writes=[B_c2])
        for d_ in range(2):
            S.op("act", lambda d_=d_: A.activation(out=lnc1[:, d_, :], in_=lbc[:, d_, 0, :], func=AF.Ln), reads=[B_c2], writes=[B_c2])
        S.op("dve", lambda: V.memset(epsc[:, 0:1], RMS_EPS), writes=[B_c2])
        S.op("dve", lambda: V.memset(epsc[:, 1:2], LN_EPS), writes=[B_c2])
        S.op("dve", lambda: V.tensor_copy(out=nps[:], in_=prmt[:, PRM_PS:PRM_PS + 4]), reads=[B_const], writes=[B_c2])
        S.op("dve", lambda: V.tensor_scalar(out=shr[:, 0, :], in0=prmt[:, PRM_L1G:PRM_L1G + 8], scalar1=ALPHA, scalar2=None, op0=ALU.mult),
             reads=[B_const], writes=[B_c2])
        S.op("dve", lambda: V.tensor_scalar(out=shr[:, 1, :], in0=prmt[:, PRM_L1B:PRM_L1B + 8], scalar1=ALPHA, scalar2=None, op0=ALU.mult),
             reads=[B_const], writes=[B_c2])

        mps, mps_b = bank()
        scv3 = scv[:].rearrange("p (a b) -> p a b", b=2)
        def mod_load(u):
            si = u % NSLOT
            S.dma("sp", slot_sem[si], slots[si][:].rearrange("p (k c) -> p k c", c=256),
                  w_mod[:, u * 256:(u + 1) * 256].rearrange("(k p) c -> p k c", p=128), writes=[slot_b[si]])

        for u in range(3):
            mod_load(u)
        for u in range(24):
            si = u % NSLOT
            if u + 3 < 24:
                mod_load(u + 3)
            wv = slots[si][:].rearrange("p (k c) -> p k c", c=256)
            for jj in range(2):
                j = u * 2 + jj
                for kc in range(8):
                    S.op("pe", lambda j=j, kc=kc, jj=jj, wv=wv: PE.matmul(
                        mps[:, j * 2:(j + 1) * 2], lhsT=wv[:, kc, jj * 128:(jj + 1) * 128], rhs=scv3[:, kc, :],
                        start=(kc == 0), stop=(kc == 7)), reads=[slot_b[si], B_c2], writes=[mps_b])
        mp3 = mps[:, 0:96].rearrange("p (j w) -> p j w", w=2)
        for w_ in range(2):
            S.op("dve", lambda w_=w_: V.tensor_tensor(out=modv[:, :, w_], in0=mp3[:, :, w_], in1=bmt[:], op=ALU.add),
                 reads=[B_const], writes=[mps_b, B_c2])
            if w_ == 1:
                done(mps_b)
            S.op("dve", lambda w_=w_: V.tensor_scalar(out=der[:, w_, 0, :], in0=modv[:, 8:16, w_], scalar1=1.0, scalar2=None, op0=ALU.add),
                 reads=[B_c2], writes=[B_c2])
            S.op("dve", lambda w_=w_: V.tensor_scalar(out=der[:, w_, 2, :], in0=modv[:, 32:40, w_], scalar1=1.0, scalar2=None, op0=ALU.add),
                 reads=[B_c2], writes=[B_c2])
            S.op("dve", lambda w_=w_: V.tensor_tensor(out=der[:, w_, 1, :], in0=der[:, w_, 2, :], in1=prmt[:, PRM_L1G:PRM_L1G + 8], op=ALU.mult),
                 reads=[B_c2], writes=[B_c2])
            S.op("dve", lambda w_=w_: V.tensor_tensor(out=der[:, w_, 2, :], in0=der[:, w_, 2, :], in1=prmt[:, PRM_L1B:PRM_L1B + 8], op=ALU.mult),
                 reads=[B_c2], writes=[B_c2])
            S.op("dve", lambda w_=w_: V.tensor_tensor(out=der[:, w_, 2, :], in0=der[:, w_, 2, :], in1=modv[:, 24:32, w_], op=ALU.add),
                 reads=[B_c2], writes=[B_c2])

        cast(w_out_b, w_out, D, D, "out")
        cast(wg_b, wg, D, DFF, "g")
        cast(wu_b, wu, D, DFF, "u")
        cast(wd_b, wd, DFF, D, "d")

        def spec_of(ident):
            kind_ = ident[0]
            if kind_ == "in":
                u = ident[1]
                return (w_in_b[:, u * 512:(u + 1) * 512].rearrange("(k p) c -> p k c", p=128), (8, 512), "in")
            if kind_ == "out":
                u = ident[1]
                return (w_out_b[:, u * 512:(u + 1) * 512].rearrange("(k p) c -> p k c", p=128), (8, 512), "out")
            if kind_ in ("g", "u"):
                u = ident[1]
                wb_ = wg_b if kind_ == "g" else wu_b
                c0 = u * 512
                c1 = min(DFF, c0 + 512)
                return (wb_[:, c0:c1].rearrange("(k p) c -> p k c", p=128), (8, c1 - c0), kind_)
            jp, hf = ident[1], ident[2]
            return (wd_b[hf * 1408:(hf + 1) * 1408, jp * 256:(jp + 1) * 256].rearrange("(k p) c -> p k c", p=128), (11, 256), "d")

        recording = order is None
        wq = [] if recording else list(order)
        wstate = {"issued": 0, "used": 0}

        def w_issue():
            while wstate["issued"] < len(wq) and wstate["issued"] < max(wstate["used"] - 1, 0) + NSLOT - 1:
                i = wstate["issued"]
                src_, (kk, cc), key = spec_of(wq[i])
                si = i % NSLOT
                dst = slots[si][:].bitcast(BF16)[:, 0:kk * cc].rearrange("p (k c) -> p k c", c=cc)
                S.dma("sp", slot_sem[si], dst, src_, reads=[B_wb[key]], writes=[slot_b[si]])
                wstate["issued"] += 1

        def w_next(*ident):
            i = wstate["used"]
            if recording:
                wq.append(ident)
            assert tuple(wq[i]) == tuple(ident), (i, wq[i], ident)
            src_, (kk, cc), key = spec_of(wq[i])
            wstate["used"] += 1
            w_issue()
            assert wstate["issued"] > i
            si = i % NSLOT
            return slots[si][:].bitcast(BF16)[:, 0:kk * cc].rearrange("p (k c) -> p k c", c=cc), slot_b[si]

        NLB = LAT_T // TB
        plan = []
        for b in reversed(range(NLB)):
            plan.append(("pre", b))
        for b in range(NLB):
            plan.append(("lat", b))
        for s_ in range(NPR):
            plan.append(("pr", s_))

        def issue_x(tok0):
            for j in range(2):
                S.dma("pool", xin_sem[j], xin[j][:], xs[tok0 + j * 128: tok0 + (j + 1) * 128, :], writes=[xin_b[j]])

        def load_x(tok0, nxt_tok0):
            pst = []
            for kp in range(4):
                ps, pb = bank()
                for kk in range(2):
                    kc = 2 * kp + kk
                    for j in range(2):
                        S.op("pe", lambda ps=ps, kk=kk, j=j, kc=kc: PE.transpose(
                            ps[:, kk * 256 + j * 128: kk * 256 + (j + 1) * 128], xin[j][:, kc * 128:(kc + 1) * 128],
                            cft[:, CF_ID:CF_ID + 128]), reads=[xin_b[j], B_const], writes=[pb])
                pst.append((ps, pb))
            if nxt_tok0 is not None:
                issue_x(nxt_tok0)
            return pst

        def evac_x(pst, w_, want_xta):
            for kp in range(4):
                ps, pb = pst[kp]
                for kk in range(2):
                    kc = 2 * kp + kk
                    if want_xta or kc % 2 == 0:
                        S.op("act", lambda ps=ps, kk=kk, kc=kc: A.activation(
                            out=hT[:, kc, :], in_=ps[:, kk * 256:(kk + 1) * 256], func=AF.Identity,
                            scale=der[:, w_, 0, kc:kc + 1], bias=modv[:, kc, w_:w_ + 1]), reads=[B_c2], writes=[pb, hT_b[kc]])
                    else:
                        S.op("dve", lambda ps=ps, kk=kk, kc=kc: V.tensor_scalar(
                            out=hT[:, kc, :], in0=ps[:, kk * 256:(kk + 1) * 256], scalar1=der[:, w_, 0, kc:kc + 1], scalar2=modv[:, kc, w_:w_ + 1],
                            op0=ALU.mult, op1=ALU.add), reads=[B_c2], writes=[pb, hT_b[kc]])
                    if want_xta:
                        S.op("dve", lambda ps=ps, kk=kk, kc=kc: V.tensor_scalar(
                            out=xTa[:, kc, :], in0=ps[:, kk * 256:(kk + 1) * 256], scalar1=ALPHA, scalar2=None, op0=ALU.mult),
                            writes=[pb, xTa_b[kc]])
                done(pb)

        def proj_fm(wv, wb, jc, ps_ap, pb):
            for kc in range(8):
                S.op("pe", lambda kc=kc: PE.matmul(ps_ap, lhsT=wv[:, kc, jc * 128:(jc + 1) * 128], rhs=hT[:, kc, :],
                                                  start=(kc == 0), stop=(kc == 7)), reads=[wb, hT_b[kc]], writes=[pb])

        def proj_tm(wv, wb, dst, dst_b, eng="act"):
            for j in range(2):
                ps, pb = bank()
                for kc in range(8):
                    S.op("pe", lambda kc=kc, j=j, ps=ps: PE.matmul(ps[:, :], lhsT=hT[:, kc, j * 128:(j + 1) * 128], rhs=wv[:, kc, :],
                                                                start=(kc == 0), stop=(kc == 7)), reads=[wb, hT_b[kc]], writes=[pb])
                db = dst_b[j] if isinstance(dst_b, list) else dst_b
                dbl = db if isinstance(db, list) else [db]
                if eng == "act":
                    S.op("act", lambda j=j, ps=ps: A.activation(out=dst[:, j, :], in_=ps[:, :], func=AF.Copy), writes=[pb] + dbl)
                else:
                    S.op("dve", lambda j=j, ps=ps: V.tensor_copy(out=dst[:, j, :], in_=ps[:, :]), writes=[pb] + dbl)
                done(pb)

        gctr = [0]

        def gates_tanh(ps_ap, pb, h, d_):
            i = d_ * 4 + h
            S.op("act", lambda: A.activation(out=tb[:, i, :], in_=ps_ap, func=AF.Tanh, scale=0.5), writes=[pb, tb_b[i]])
            done(pb)

        gslot = {}

        def gates_ln(h, d_):
            i = d_ * 4 + h
            k0 = (3 * gctr[0]) % 6
            gctr[0] += 1
            s_l = k0
            gslot[(h, d_)] = (k0, k0 + 1, k0 + 2)
            S.op("act", lambda: A.activation(out=gw[:, s_l, :], in_=tb[:, i, :], func=AF.Ln, scale=lbc[:, d_, 0, h:h + 1], bias=lbc[:, d_, 1, h:h + 1]),
                 reads=[tb_b[i], B_c2], writes=[gw_b[s_l]])

        def gates_all(pairs, want_q):
            gates_ln(*pairs[0])
            for n_, (h, d_) in enumerate(pairs):
                if n_ + 1 < len(pairs):
                    gates_ln(*pairs[n_ + 1])
                gates_chain(h, d_, want_q)

        def gates_chain(h, d_, want_q):
            i = d_ * 4 + h
            s_l, s_b, s_r = gslot[(h, d_)]
            if d_ == 0:
                mk = cft[:, CF_NST:CF_NST + TB]
                S.op("dve", lambda: V.tensor_tensor_scan(out=gw[:, s_b, :], data0=mk, data1=gw[:, s_l, :], initial=0.0,
                                                         op0=ALU.mult, op1=ALU.add), reads=[gw_b[s_l], B_const], writes=[gw_b[s_b]])
            else:
                mk = cft[:, CF_NEN:CF_NEN + TB]
                S.op("dve", lambda: V.tensor_tensor_scan(out=gw[:, s_b, ::-1], data0=mk[:, ::-1], data1=gw[:, s_l, ::-1], initial=0.0,
                                                         op0=ALU.mult, op1=ALU.add), reads=[gw_b[s_l], B_const], writes=[gw_b[s_b]])
            S.op("act", lambda: A.activation(out=gw[:, s_l, :], in_=gw[:, s_b, :], func=AF.Exp), reads=[gw_b[s_b]], writes=[gw_b[s_l]])
            S.op("act", lambda: A.activation(out=gw[:, s_r, :], in_=gw[:, s_b, :], func=AF.Exp, scale=-1.0, bias=lnc1[:, d_, h:h + 1]),
                 reads=[gw_b[s_b], B_c2], writes=[gw_b[s_r]])
            ecols = gw[:, s_l, CH - 1::CH] if d_ == 0 else gw[:, s_l, 0::CH]
            S.op("pool", lambda: G.tensor_copy(out=edg[:, d_, h, :], in_=ecols), reads=[gw_b[s_l]], writes=[edg_b[d_][h]])
            S.op("dve", lambda: V.scalar_tensor_tensor(out=kh[:, d_, h, :], in0=tb[:, i, :], scalar=1.0, in1=gw[:, s_r, :],
                                                       op0=ALU.subtract, op1=ALU.mult), reads=[tb_b[i], gw_b[s_r]], writes=[kh_b[d_][h]])
            if want_q:
                S.op("pool", lambda: G.tensor_tensor(out=qh[:, d_, h, :], in0=sq[h][:], in1=gw[:, s_l, :], op=ALU.mult),
                     reads=[sq_b[h], gw_b[s_l]], writes=[qh_b[d_][h]])

        def khT_make(h, d_):
            ps, pb = bank()
            psb = ps[:].bitcast(BF16)
            for j in range(2):
                S.op("pe", lambda j=j: PE.transpose(psb[:, j * 128:(j + 1) * 128], kh[:, d_, h, j * 128:(j + 1) * 128], identb[:]),
                     reads=[kh_b[d_][h], B_c2], writes=[pb])
            S.op("act", lambda: A.activation(out=khT[:, d_, h, :, :], in_=psb[:, 0:256].rearrange("p (j c) -> p j c", c=128), func=AF.Copy),
                 writes=[pb, khT_b[d_][h]])
            done(pb)

        def scores(h, d_):
            ps, pb = bank()
            for m in range(NCHK):
                j, a = m // 2, m % 2
                S.op("pe", lambda m=m, j=j, a=a: PE.matmul(
                    ps[a * 64:(a + 1) * 64, j * 64:(j + 1) * 64], lhsT=kh[:, d_, h, m * 64:(m + 1) * 64], rhs=qh[:, d_, h, m * 64:(m + 1) * 64],
                    start=True, stop=True), reads=[kh_b[d_][h], qh_b[d_][h]], writes=[pb])
            mk = cft[:, CF_MF:CF_MF + 128] if d_ == 0 else cft[:, CF_MB:CF_MB + 128]
            S.op("dve", lambda: V.tensor_tensor(out=scs[:, d_, h, :, :].rearrange("p j c -> p (j c)"), in0=ps[:, 0:128], in1=mk, op=ALU.mult),
                 reads=[B_const], writes=[pb, scs_b[d_][h]])
            done(pb)

        def w_mm(h, d_, m):
            ps, pb = bank()
            j, a = m // 2, m % 2
            S.op("pe", lambda: PE.matmul(ps[:, 0:128], lhsT=khT[a * 64:(a + 1) * 64, d_, h, j, :], rhs=vT[a * 64:(a + 1) * 64, j, h * 128:(h + 1) * 128],
                                         start=True, stop=True), reads=[khT_b[d_][h], vT_b[j]], writes=[pb])
            return ps, pb

        def recur(h, d_, step, m_prev, t0, t0_b, sb_out, sb_out_b, psw, pbw):
            if step == 0:
                tprev, rd, sc = t0, [t0_b], 1.0
            else:
                tprev = Tst[:, d_, h, (step - 1) % 2, :]
                rd = [Tst_b[d_][h][(step - 1) % 2], edg_b[d_][h]]
                sc = edg[:, d_, h, m_prev:m_prev + 1]
            if sb_out is not None:
                S.op("act", lambda: A.activation(out=sb_out, in_=tprev, func=AF.Identity, scale=sc), reads=rd, writes=[sb_out_b])
            S.op("dve", lambda: V.scalar_tensor_tensor(out=Tst[:, d_, h, step % 2, :], in0=tprev, scalar=sc, in1=psw[:, 0:128],
                                                       op0=ALU.mult, op1=ALU.subtract), reads=rd, writes=[pbw, Tst_b[d_][h][step % 2]])
            done(pbw)

        def final_state(h, d_, m_last, dst, dst_b):
            li = (NCHK - 1) % 2
            S.op("act", lambda: A.activation(out=dst, in_=Tst[:, d_, h, li, :], func=AF.Identity, scale=edg[:, d_, h, m_last:m_last + 1]),
                 reads=[Tst_b[d_][h][li], edg_b[d_][h]], writes=[dst_b])

        def pre_block(b, tok0, nxt_tok0):
            pst = load_x(tok0, nxt_tok0)
            evac_x(pst, 0, False)
            wv, wb = w_next("in", 2)
            for h in range(4):
                ps, pb = bank()
                proj_fm(wv, wb, h, ps[:, 0:TB], pb)
                gates_tanh(ps[:, 0:TB], pb, h, 1)
            wv, wb = w_next("in", 3)
            proj_tm(wv, wb, vT, vT_b)
            wv, wb = w_next("in", 5)
            proj_tm(wv, wb, upT, upT_b, eng="dve")
            S.dma("act", ups_sem, ups[tok0:tok0 + TB, :].rearrange("(j p) f -> p j f", p=128), upT, reads=scs_all, writes=[B_ups])
            gates_all([(h, 1) for h in range(4)], False)
            for h in range(4):
                khT_make(h, 1)
            for step, m in enumerate(reversed(range(NCHK))):
                for h in range(4):
                    psw, pbw = w_mm(h, 1, m)
                    recur(h, 1, step, m + 1, carryB[:, h, :], carryB_b[h], None, None, psw, pbw)
            for h in range(4):
                final_state(h, 1, 0, carryB[:, h, :], carryB_b[h])
            if b > 0:
                S.dma("act", bst_sem, bst[b - 1], carryB[:], reads=carryB_b, writes=[B_bst[b - 1]])

        def layer_norm(src_ps_fn, res, res_b, gate_col, outs):
            for j in range(8):
                ps, pb = src_ps_fn(j)
                S.op("dve", lambda ps=ps, j=j: V.scalar_tensor_tensor(out=rr[:, j, :], in0=ps, scalar=gate_col(j), in1=res[:, j, :],
                                                                  op0=ALU.mult, op1=ALU.add), reads=[res_b[j], B_c2], writes=[pb, rr_b[j]])
                done(pb)
                S.op("dve", lambda j=j: V.tensor_copy(out=rbb[:, j, :], in_=rr[:, j, :]), reads=[rr_b[j]], writes=[rbb_b[j]])
                S.op("act", lambda j=j: A.activation(out=rsq[:, j, :], in_=rr[:, j, :], func=AF.Square), reads=[rr_b[j]], writes=[rsq_b[j]])
            pmn, pmnb = bank()
            for j in range(8):
                S.op("pe", lambda j=j: PE.matmul(pmn[:, 0:TB], lhsT=onesb[:, 1, :], rhs=rbb[:, j, :], start=(j == 0), stop=(j == 7)),
                     reads=[rbb_b[j], B_c2], writes=[pmnb])
            pex, pexb = bank()
            for j in range(8):
                S.op("pe", lambda j=j: PE.matmul(pex[:, 0:TB], lhsT=onesb[:, 1, :], rhs=rsq[:, j, :], start=(j == 0), stop=(j == 7)),
                     reads=[rsq_b[j], B_c2], writes=[pexb])
            S.op("act", lambda: A.activation(out=stt[:, 0, :], in_=pmn[:, 0:TB], func=AF.Square), writes=[pmnb, stt_b[0]])
            S.op("dve", lambda: V.tensor_tensor(out=stt[:, 0, :], in0=pex[:, 0:TB], in1=stt[:, 0, :], op=ALU.subtract), writes=[pexb, stt_b[0]])
            done(pexb)
            S.op("act", lambda: A.activation(out=stt[:, 0, :], in_=stt[:, 0, :], func=AF.Ln, bias=epsc[:, 1:2]), reads=[B_c2], writes=[stt_b[0]])
            S.op("act", lambda: A.activation(out=stt[:, 1, :], in_=stt[:, 0, :], func=AF.Exp, scale=-0.5), reads=[stt_b[0]], writes=[stt_b[1]])
            S.op("dve", lambda: V.scalar_tensor_tensor(out=stt[:, 2, :], in0=pmn[:, 0:TB], scalar=-1.0, in1=stt[:, 1, :], op0=ALU.mult, op1=ALU.mult),
                 reads=[stt_b[1]], writes=[pmnb, stt_b[2]])
            done(pmnb)
            for j in range(8):
                S.op("dve", lambda j=j: V.tensor_tensor(out=rr[:, j, :], in0=rr[:, j, :], in1=stt[:, 1, :], op=ALU.mult),
                     reads=[stt_b[1]], writes=[rr_b[j]])
                S.op("pool", lambda j=j: G.tensor_tensor(out=rr[:, j, :], in0=rr[:, j, :], in1=stt[:, 2, :], op=ALU.add),
                     reads=[stt_b[2]], writes=[rr_b[j]])
                for (dst, dstb, scf, bif) in outs:
                    S.op("act", lambda j=j, dst=dst, scf=scf, bif=bif: A.activation(out=dst[:, j, :], in_=rr[:, j, :], func=AF.Identity,
                                                                               scale=scf(j), bias=bif(j)),
                         reads=[rr_b[j], B_c2, B_const], writes=[dstb[j]])

        prefetched = set()

        def fetch_lat(b):
            if ("lat", b) in prefetched:
                return
            prefetched.add(("lat", b))
            if b == NLB - 1:
                S.dma("pool", binit_sem, binit[:], st0[:, 1], writes=[binit_b])
            else:
                S.dma("pool", binit_sem, binit[:], bst[b], reads=[B_bst[b]], writes=[binit_b])
            n_lo, n_hi = max(0, 2 * b - 4), min(32, 2 * b + 6)
            S.dma("sp", uwin_sem, uwin[:, n_lo - (2 * b - 4): n_hi - (2 * b - 4), :],
                  ups[n_lo * 128:n_hi * 128, :].rearrange("(j p) f -> p j f", p=128), reads=[B_ups], writes=[uwin_b])

        def pool_branch(lat, b):
            for g in range(4):
                gi2 = g % 2
                pp, ppb = bank()
                for ml in range(2):
                    if lat:
                        m = 2 * b + ml
                        ins_ = [(_IDX2[(g, dl)], m + dl - (2 * b - 4)) for dl in range(-5, 6) if (g, dl) in _IDX2 and 0 <= m + dl < 32]
                    else:
                        ins_ = [(_IDX1[(g, dl)], ml + dl) for dl in (-1, 0, 1) if (g, dl) in _IDX1 and 0 <= ml + dl < 2]
                    for ii, (pi, ui) in enumerate(ins_):
                        S.op("pe", lambda ml=ml, pi=pi, ui=ui, ii=ii, nn=len(ins_): PE.matmul(
                            pp[:, ml * 128:(ml + 1) * 128], lhsT=ptt[:, pi, :], rhs=uwin[:, ui, g * 128:(g + 1) * 128],
                            start=(ii == 0), stop=(ii == nn - 1)), reads=[uwin_b, B_const], writes=[ppb])
                for ml in range(2):
                    ri = _R2[(g, 2 * b + ml)] if lat else _R1[(g, ml)]
                    ui = (ml + 4) if lat else ml
                    S.op("dve", lambda ml=ml, ri=ri, ui=ui: V.scalar_tensor_tensor(
                        out=dtm[gi2][:, ml, :], in0=pp[:, ml * 128:(ml + 1) * 128], scalar=rct[:, ri:ri + 1], in1=uwin[:, ui, g * 128:(g + 1) * 128],
                        op0=ALU.mult, op1=ALU.subtract), reads=[B_const, uwin_b], writes=[ppb, dtm_b[gi2]])
                done(ppb)
                pt_, ptb = bank()
                ptb16 = pt_[:].bitcast(BF16)
                for ml in range(2):
                    S.op("pe", lambda ml=ml: PE.transpose(ptb16[:, ml * 128:(ml + 1) * 128], dtm[gi2][:, ml, :], identb[:]),
                         reads=[dtm_b[gi2], B_c2], writes=[ptb])
                S.op("act", lambda: A.activation(out=ndf[gi2][:], in_=ptb16[:, 0:TB], func=AF.Copy), writes=[ptb, ndf_b[gi2]])
                done(ptb)
                py, pyb = bank()
                S.op("pe", lambda: PE.matmul(py[:, 0:TB], lhsT=poolwb[:, g, :], rhs=ndf[gi2][:], start=True, stop=True),
                     reads=[ndf_b[gi2], B_c2], writes=[pyb])
                S.op("act", lambda: A.activation(out=ymix[:, 4 + g, :], in_=py[:, 0:TB], func=AF.Identity, scale=nps[:, g:g + 1]),
                     reads=[B_c2], writes=[pyb, ymix_b[4 + g]])
                done(pyb)

        def A_gen(kind, b, tok0, nxt_tok0, nxt):
            lat = kind == "lat"
            w_ = 0 if lat else 1
            if lat:
                fetch_lat(b)
                if b == 0:
                    S.dma("sp", misc_sem, carryF[:], st0[:, 0], writes=carryF_b)
            pst = load_x(tok0, nxt_tok0)
            evac_x(pst, w_, True)
            yield
            wv, wb = w_next("in", 0)
            for h in range(4):
                ps, pb = bank()
                proj_fm(wv, wb, h, ps[:, 0:TB], pb)
                S.op("act", lambda ps=ps, h=h: A.activation(out=sq[h][:], in_=ps[:, 0:TB], func=AF.Silu), writes=[pb, sq_b[h]])
                done(pb)
            yield
            for d_ in range(2):
                wv, wb = w_next("in", 1 + d_)
                for h in range(4):
                    ps, pb = bank()
                    proj_fm(wv, wb, h, ps[:, 0:TB], pb)
                    gates_tanh(ps[:, 0:TB], pb, h, d_)
                yield
            wv, wb = w_next("in", 3)
            proj_tm(wv, wb, vT, vT_b)
            yield
            wv, wb = w_next("in", 4)
            for h in range(4):
                ps, pb = bank()
                proj_fm(wv, wb, h, ps[:, 0:TB], pb)
                S.op("act", lambda ps=ps, h=h: A.activation(out=sgate[:, h, :], in_=ps[:, 0:TB], func=AF.Silu), writes=[pb, sgate_b[h]])
                done(pb)
            if not lat:
                wv, wb = w_next("in", 5)
                proj_tm(wv, wb, uwin, uwin_b)
            yield
            pool_branch(lat, b)
            yield
            gates_all([(h, d_) for d_ in range(2) for h in range(4)], True)
            yield
            for d_ in range(2):
                for h in range(4):
                    khT_make(h, d_)
                    scores(h, d_)
                yield
            ni = b % 2
            for step, m in enumerate(reversed(range(NCHK))):
                for h in range(4):
                    t0, t0_b = (binit[:, h, :], binit_b) if lat else (zst[:], B_c2)
                    psw, pbw = w_mm(h, 1, m)
                    recur(h, 1, step, m + 1, t0, t0_b, Sbb[:, h, m, :], Sbb_b[h][m], psw, pbw)
                if step < NCHK - 1:
                    yield
            if not lat:
                for h in range(4):
                    final_state(h, 1, 0, nsst[ni][:, h, :], nsst_hb(ni, h))
            yield
            pos = [bank() for h in range(4)]
            for m in range(NCHK):
                j, a = m // 2, m % 2
                for h in range(4):
                    po, pob = pos[h]
                    t0, t0_b = (carryF[:, h, :], carryF_b[h]) if lat else (zst[:], B_c2)
                    psw, pbw = w_mm(h, 0, m)
                    recur(h, 0, m, m - 1, t0, t0_b, Sbf[:, h, m % 2, :], Sbf_b[h][m % 2], psw, pbw)
                    oc = po[:, m * 64:(m + 1) * 64]
                    S.op("pe", lambda m=m, oc=oc, h=h: PE.matmul(oc, lhsT=Sbf[:, h, m % 2, :], rhs=qh[:, 0, h, m * 64:(m + 1) * 64], start=True, stop=False),
                         reads=[Sbf_b[h][m % 2], qh_b[0][h]], writes=[pob])
                    S.op("pe", lambda m=m, oc=oc, h=h: PE.matmul(oc, lhsT=vT[a * 64:(a + 1) * 64, j, h * 128:(h + 1) * 128],
                                                               rhs=scs[a * 64:(a + 1) * 64, 0, h, j, :], start=False, stop=False),
                         reads=[vT_b[j], scs_b[0][h]], writes=[pob])
                    S.op("pe", lambda m=m, oc=oc, h=h: PE.matmul(oc, lhsT=Sbb[:, h, m, :], rhs=qh[:, 1, h, m * 64:(m + 1) * 64], start=False, stop=False),
                         reads=[Sbb_b[h][m], qh_b[1][h]], writes=[pob])
                    S.op("pe", lambda m=m, oc=oc, h=h: PE.matmul(oc, lhsT=vT[a * 64:(a + 1) * 64, j, h * 128:(h + 1) * 128],
                                                               rhs=scs[a * 64:(a + 1) * 64, 1, h, j, :], start=False, stop=True),
                         reads=[vT_b[j], scs_b[1][h]], writes=[pob])
                yield
            for h in range(4):
                if h == 2:
                    yield
                po, pob = pos[h]
                final_state(h, 0, NCHK - 1, carryF[:, h, :], carryF_b[h])
                S.op("act", lambda h=h, po=po: A.activation(out=osb[h][:], in_=po[:, 0:TB], func=AF.Copy), writes=[pob, osb_b[h]])
                S.op("act", lambda h=h, po=po: A.activation(out=osq[h % 2][:], in_=po[:, 0:TB], func=AF.Square), writes=[pob, osq_b[h % 2]])
                done(pob)
                pm, pmb = bank()
                S.op("pe", lambda h=h, pm=pm: PE.matmul(pm[:, 0:TB], lhsT=onesb[:, 0, :], rhs=osq[h % 2][:], start=True, stop=True),
                     reads=[osq_b[h % 2], B_c2], writes=[pmb])
                S.op("act", lambda h=h, pm=pm: A.activation(out=rstd_t[:, h, :], in_=pm[:, 0:TB], func=AF.Ln, bias=epsc[:, 0:1]),
                     reads=[B_c2], writes=[pmb, rstd_b])
                done(pmb)
                S.op("dve", lambda h=h: V.scalar_tensor_tensor(out=osb[h][:], in0=osb[h][:], scalar=prmt[:, PRM_GN:PRM_GN + 1], in1=sgate[:, h, :],
                                                           op0=ALU.mult, op1=ALU.mult), reads=[B_const, sgate_b[h]], writes=[osb_b[h]])
            if nxt is not None and nxt[0] == "lat":
                fetch_lat(nxt[1])
            yield
            S.op("act", lambda: A.activation(out=rstd_t[:], in_=rstd_t[:], func=AF.Exp, scale=-0.5), writes=[rstd_b])
            for h in range(4):
                S.op("dve", lambda h=h: V.tensor_tensor(out=ymix[:, h, :], in0=osb[h][:], in1=rstd_t[:, h, :], op=ALU.mult),
                     reads=[osb_b[h], rstd_b], writes=[ymix_b[h]])
            if not lat:
                S.dma("act", nsf_sem, nsf[b].rearrange("h d e -> d h e"), carryF[:], reads=carryF_b)
                S.dma("act", nsb_sems[ni], nsb[b].rearrange("h d e -> d h e"), nsst[ni][:], reads=(carryB_b if ni == 0 else [binit_b]))

        def F_wout_ln1(w_):
            wo = {}

            def wout_ps(j):
                if j // 4 not in wo:
                    wo[j // 4] = w_next("out", j // 4)
                ps, pb = bank()
                wv, wb = wo[j // 4]
                jc = j % 4
                for kc in range(8):
                    S.op("pe", lambda kc=kc: PE.matmul(ps[:, 0:TB], lhsT=wv[:, kc, jc * 128:(jc + 1) * 128], rhs=ymix[:, kc, :],
                                                      start=(kc == 0), stop=(kc == 7)), reads=[wb, ymix_b[kc]], writes=[pb])
                return ps[:, 0:TB], pb

            layer_norm(wout_ps, xTa, xTa_b, lambda j: modv[:, 16 + j, w_:w_ + 1],
                       [(h2, h2_b, lambda j: der[:, w_, 1, j:j + 1], lambda j: der[:, w_, 2, j:j + 1]),
                        (x2a, x2a_b, lambda j: shr[:, 0, j:j + 1], lambda j: shr[:, 1, j:j + 1])])

        gu_open = {}

        def F_gu_jj(u, jl):
            if jl == 0:
                gu_open["g"] = w_next("g", u)
                gu_open["u"] = w_next("u", u)
            wgv, wgb_ = gu_open["g"]
            wuv, wub_ = gu_open["u"]
            jj = u * 4 + jl
            ps, pb = bank()
            for kc in range(8):
                S.op("pe", lambda kc=kc: PE.matmul(ps[:, 0:TB], lhsT=wgv[:, kc, jl * 128:(jl + 1) * 128], rhs=h2[:, kc, :],
                                                  start=(kc == 0), stop=(kc == 7)), reads=[wgb_, h2_b[kc]], writes=[pb])
            ps2, pb2 = bank()
            for kc in range(8):
                S.op("pe", lambda kc=kc: PE.matmul(ps2[:, 0:TB], lhsT=wuv[:, kc, jl * 128:(jl + 1) * 128], rhs=h2[:, kc, :],
                                                  start=(kc == 0), stop=(kc == 7)), reads=[wub_, h2_b[kc]], writes=[pb2])
            gi3 = jj % 2
            S.op("act", lambda: A.activation(out=sgt[gi3][:], in_=ps[:, 0:TB], func=AF.Silu), writes=[pb, sgt_b[gi3]])
            done(pb)
            S.op("dve", lambda: V.tensor_tensor(out=act[:, jj, :], in0=ps2[:, 0:TB], in1=sgt[gi3][:], op=ALU.mult),
                 reads=[sgt_b[gi3]], writes=[pb2, act_b[jj]] + stg_b)
            done(pb2)

        def F_dn_ln2(w_, hook):
            dn_cache = {}

            def down_ps(j):
                if j > 0:
                    hook()
                jp, jl = j // 2, j % 2
                if jp not in dn_cache:
                    dn_cache.clear()
                    dn_cache[jp] = [w_next("d", jp, 0), w_next("d", jp, 1)]
                ps, pb = bank()
                for hf in range(2):
                    wv, wb = dn_cache[jp][hf]
                    for kl in range(11):
                        kk = hf * 11 + kl
                        S.op("pe", lambda kl=kl, kk=kk, wv=wv: PE.matmul(ps[:, 0:TB], lhsT=wv[:, kl, jl * 128:(jl + 1) * 128], rhs=act[:, kk, :],
                                                                       start=(kk == 0), stop=(kk == 21)), reads=[wb, act_b[kk]], writes=[pb])
                return ps[:, 0:TB], pb

            layer_norm(down_ps, x2a, x2a_b, lambda j: modv[:, 40 + j, w_:w_ + 1],
                       [(outF, outF_b, lambda j: prmt[:, PRM_L2G + j:PRM_L2G + j + 1], lambda j: prmt[:, PRM_L2B + j:PRM_L2B + j + 1])])

        def F_out(tok0):
            for j in range(2):
                for half in range(2):
                    ps, pb = bank()
                    for kl in range(4):
                        kc = half * 4 + kl
                        S.op("pe", lambda kl=kl, kc=kc: PE.transpose(ps[:, kl * 128:(kl + 1) * 128], outF[:, kc, j * 128:(j + 1) * 128],
                                                                    cft[:, CF_ID:CF_ID + 128]), reads=[outF_b[kc], B_const], writes=[pb])
                    S.op("act", lambda: A.activation(out=stg[j][:, half * 512:(half + 1) * 512], in_=ps[:, :], func=AF.Copy),
                         writes=[pb, stg_b[j]] + act_b)
                    done(pb)
                S.dma("act", stg_sem[j], y[tok0 + j * 128: tok0 + (j + 1) * 128, :], stg[j][:], reads=[stg_b[j]])

        S.dma("sp", misc_sem, carryB[:], st0[:, 1], writes=carryB_b)
        w_issue()

        def tok_of(kind, b):
            return b * TB if kind in ("pre", "lat") else LAT_T + b * PR_T

        issue_x(tok_of(*plan[0]))
        pre = [p for p in plan if p[0] == "pre"]
        mains = [p for p in plan if p[0] != "pre"]
        for pi_, (kind, b) in enumerate(pre):
            nxt = plan[pi_ + 1]
            pre_block(b, tok_of(kind, b), tok_of(*nxt))

        def mk_A(i):
            kind, b = mains[i]
            nxt = mains[i + 1] if i + 1 < len(mains) else None
            return A_gen(kind, b, tok_of(kind, b), tok_of(*nxt) if nxt is not None else None, nxt)

        def drain(gen, n=None):
            k = 0
            while gen is not None and (n is None or k < n):
                try:
                    next(gen)
                except StopIteration:
                    return None
                k += 1
            return gen

        drain(mk_A(0))
        for i, (kind, b) in enumerate(mains):
            w_ = 0 if kind == "lat" else 1
            F_wout_ln1(w_)
            if i > 0:
                F_out(tok_of(*mains[i - 1]))
            hold = [mk_A(i + 1) if i + 1 < len(mains) else None]

            def adv(n=1):
                hold[0] = drain(hold[0], n)

            adv(2)
            for u in range(6):
                for jl in range(4 if u < 5 else 2):
                    F_gu_jj(u, jl)
                    if u >= 2:
                        adv()
                if u < 2:
                    adv(2)
            F_dn_ln2(w_, adv)
            drain(hold[0])
        F_out(tok_of(*mains[-1]))
        for s_ in stg_sem + [nsf_sem] + nsb_sems:
            SP.wait_ge(S.sems[s_], S.cnt[s_])
        build_program.stats = (dict(S.cnt), S.nwait)
        build_program.order = list(wq)
    return nc


_NC_CACHE = {}


def _prep_inputs(inp):
    f = lambda a: np.ascontiguousarray(np.asarray(a, dtype=np.float32))
    xp, xsm = f(inp["x_prompt"]), f(inp["x_sample"])
    sf, sbk = f(inp["state_hgrn_fwd"]), f(inp["state_hgrn_bwd"])
    c, cctx = f(inp["c"]), f(inp["c_ctx"])
    shared = {
        "w_mod": f(inp["w_mod"][0]),
        "b_mod": np.ascontiguousarray(f(inp["b_mod"][0]).reshape(48, 128).T),
        "w_in": f(inp["w_in"][0]),
        "w_out": f(inp["w_out"][0]),
        "wg": f(inp["w_ffn_gate"][0]),
        "wu": f(inp["w_ffn_up"][0]),
        "wd": f(inp["w_ffn_down"][0]),
        "poolw": np.ascontiguousarray(f(inp["pool_w"][0]).transpose(1, 0, 2)),
        "cf": _CF,
        "ptab": _PT,
        "rtab": _RC,
    }
    prm = np.zeros((128, PRM_N), np.float32)
    lf, lb_ = f(inp["lb_fwd_raw"]), f(inp["lb_bwd_raw"])
    prm[:, PRM_LBF:PRM_LBF + 8] = lf.reshape(2, 4, 128).transpose(2, 1, 0).reshape(128, 8)
    prm[:, PRM_LBB:PRM_LBB + 8] = lb_.reshape(2, 4, 128).transpose(2, 1, 0).reshape(128, 8)
    prm[:, PRM_GN] = f(inp["hgrn_norm_g"][0])
    prm[:, PRM_PS:PRM_PS + 4] = f(inp["pool_scale"][0]).reshape(4, 128).T
    prm[:, PRM_L1G:PRM_L1G + 8] = f(inp["ln1_g"][0]).reshape(8, 128).T
    prm[:, PRM_L1B:PRM_L1B + 8] = f(inp["ln1_b"][0]).reshape(8, 128).T
    prm[:, PRM_L2G:PRM_L2G + 8] = f(inp["ln2_g"][0]).reshape(8, 128).T
    prm[:, PRM_L2B:PRM_L2B + 8] = f(inp["ln2_b"][0]).reshape(8, 128).T
    shared["prm"] = prm
    maps = []
    for i in range(NCORES):
        m = dict(shared)
        m["xs"] = np.concatenate([xsm[i], xp[i * NPR:(i + 1) * NPR].reshape(NPR * PR_T, D)], 0)
        st = np.stack([sf[i, 0], sbk[i, 0]], 0)
        m["st0"] = np.ascontiguousarray(st.transpose(2, 0, 1, 3))
        cv = np.stack([c[i], cctx], -1)
        m["cvec"] = np.ascontiguousarray(cv.reshape(8, 128, 2).transpose(1, 0, 2))
        maps.append(m)
    return maps


def kernel(**inputs):
    if "nc" not in _NC_CACHE:
        build_program()
        _NC_CACHE["nc"] = build_program(build_program.order)
    nc = _NC_CACHE["nc"]
    maps = _prep_inputs(inputs)
    res = run_bass_kernel_spmd(nc, maps, core_ids=list(range(NCORES)))
    ys = [np.asarray(r["y"], dtype=np.float32) for r in res.results]
    y_sample = np.stack([a[:LAT_T] for a in ys], 0)
    y_prompt = np.concatenate([a[LAT_T:].reshape(NPR, PR_T, D) for a in ys], 0)
    nsf = np.concatenate([np.asarray(r["nsf"], dtype=np.float32) for r in res.results], 0)[:, None]
    nsb = np.concatenate([np.asarray(r["nsb"], dtype=np.float32) for r in res.results], 0)[:, None]
    return (y_prompt, y_sample, nsf, nsb)
```
